# Optimizing a Trainium2 kernel written in Bass

```python
import jax, jax.numpy as jnp
from jax import lax
import numpy as np

D_MODEL = 1024
BATCH = 4
SEQ = 8192
DEPTH = 1

N_META = 16
HEAD_SIZE_A = 64
N_HEADS_A = D_MODEL // HEAD_SIZE_A
WIDTH_A = N_HEADS_A * HEAD_SIZE_A
DECAY_LORA = 64
AAA_LORA = 64
GATE_LORA = 128
LN_X_EPS = 64e-5
WIDTH_B = D_MODEL
N_BLOCKS_B = 16
BLOCK_B = WIDTH_B // N_BLOCKS_B
CONV_WIDTH = 4
LRU_C = 8.0
D_FF = 4 * D_MODEL
RMS_EPS = 1e-6
RWKV_SPLITS = (WIDTH_A, 2 * WIDTH_A, 3 * WIDTH_A, 3 * WIDTH_A + DECAY_LORA, 3 * WIDTH_A + DECAY_LORA + AAA_LORA)
COLS_A = 3 * WIDTH_A + DECAY_LORA + AAA_LORA + GATE_LORA
N_IN_COLS = COLS_A + 2 * WIDTH_B + 2 * D_MODEL

kernel_name = 'hybrid_rwkv7_rglru_block'


def rmsnorm(x, g):
    xf = x.astype(jnp.float32)
    y = xf * lax.rsqrt(jnp.mean(xf * xf, axis=-1, keepdims=True) + RMS_EPS)
    return (y * g.astype(jnp.float32)).astype(x.dtype)


def token_shift(p, mu):
    prev = jnp.pad(p[:, :-1], ((0, 0), (1, 0), (0, 0)))
    return p + (prev - p) * mu


def wkv7_scan(r, w, k, v, a_vec, b_vec):
    B, _, H, N = r.shape
    xs = tuple(jnp.moveaxis(t, 1, 0) for t in (r, w, k, v, a_vec, b_vec))

    def step(S, inp):
        r_t, w_t, k_t, v_t, a_t, b_t = inp
        sa = jnp.einsum('bhvk,bhk->bhv', S, a_t)
        S = S * w_t[:, :, None, :] + sa[..., None] * b_t[:, :, None, :] + v_t[..., None] * k_t[:, :, None, :]
        y_t = jnp.einsum('bhvk,bhk->bhv', S, r_t)
        return S, y_t

    S0 = jnp.zeros((B, H, N, N), jnp.float32)
    _, y = lax.scan(step, S0, xs)
    return jnp.moveaxis(y, 0, 1)


def rwkv7_time_mix(pa, mu_shift, w0, w_decay_up, a0, w_aaa_up, w_gate_up, k_k, k_a, r_k, ln_x_w, ln_x_b):
    dt = pa.dtype
    B, T, _ = pa.shape
    f32 = jnp.float32
    xs = token_shift(pa, mu_shift)
    r, k, v, wd, ad, gd = jnp.split(xs, RWKV_SPLITS, axis=-1)
    w_log = -jax.nn.softplus(-(w0 + jnp.tanh(wd) @ w_decay_up).astype(f32)) - 0.5
    decay = jnp.exp(-jnp.exp(w_log))
    a = jax.nn.sigmoid((a0 + ad @ w_aaa_up).astype(f32))
    g = jax.nn.sigmoid(gd) @ w_gate_up
    kf = k.astype(f32)
    kk = (kf * k_k.astype(f32)).reshape(B, T, N_HEADS_A, HEAD_SIZE_A)
    kk = kk / jnp.maximum(jnp.sqrt(jnp.sum(kk * kk, axis=-1, keepdims=True)), 1e-12)
    kmod = kf * (1.0 + (a - 1.0) * k_a.astype(f32))
    heads = lambda t: t.astype(f32).reshape(B, T, N_HEADS_A, HEAD_SIZE_A)
    rh, kh, vh, ah, dh = heads(r), heads(kmod), heads(v), heads(a), heads(decay)
    y = wkv7_scan(rh, dh, kh, vh, -kk, kk * ah)
    mean = jnp.mean(y, axis=-1, keepdims=True)
    var = jnp.mean(jnp.square(y - mean), axis=-1, keepdims=True)
    y = ((y - mean) * lax.rsqrt(var + LN_X_EPS)).reshape(B, T, WIDTH_A) * ln_x_w + ln_x_b
    bonus = jnp.sum(rh * kh * r_k.astype(f32), axis=-1, keepdims=True) * vh
    y = y + bonus.reshape(B, T, WIDTH_A)
    return (y * g).astype(dt)


def _lin_combine(c1, c2):
    a1, b1 = c1
    a2, b2 = c2
    return a1 * a2, a2 * b1 + b2


def rglru_branch(xb, yb, conv_w, conv_b, lru_wa, lru_ba, lru_wx, lru_bx, lru_lambda):
    dt = xb.dtype
    B, T, _ = xb.shape
    f32 = jnp.float32
    xp = jnp.pad(xb, ((0, 0), (CONV_WIDTH - 1, 0), (0, 0)))
    xc = conv_b + sum(xp[:, j:j + T] * conv_w[j] for j in range(CONV_WIDTH))
    xh = xc.reshape(B, T, N_BLOCKS_B, BLOCK_B)
    gate_r = jax.nn.sigmoid((jnp.einsum('bthi,hij->bthj', xh, lru_wa).reshape(B, T, WIDTH_B) + lru_ba).astype(f32))
    gate_i = jax.nn.sigmoid((jnp.einsum('bthi,hij->bthj', xh, lru_wx).reshape(B, T, WIDTH_B) + lru_bx).astype(f32))
    log_a = -LRU_C * jax.nn.softplus(-lru_lambda.astype(f32)) * gate_r
    a = jnp.exp(log_a)
    b = jnp.sqrt(-jnp.expm1(2.0 * log_a)) * (gate_i * xc.astype(f32))
    _, hs = lax.associative_scan(_lin_combine, (a, b), axis=1)
    return (hs * jax.nn.gelu(yb.astype(f32))).astype(dt)


def hybrid_layer(h, norm_mix_g, w_in, mu_shift, w0, w_decay_up, a0, w_aaa_up, w_gate_up, k_k, k_a, r_k,
                 ln_x_w, ln_x_b, w_proj_a, conv_w, conv_b, lru_wa, lru_ba, lru_wx, lru_bx, lru_lambda,
                 w_proj_b, w_out, norm_ffn_g, w_ff_up, w_ff_down):
    u = rmsnorm(h, norm_mix_g)
    p = u @ w_in
    pa, xb, yb, gates = jnp.split(p, (COLS_A, COLS_A + WIDTH_B, COLS_A + 2 * WIDTH_B), axis=-1)
    ya = rwkv7_time_mix(pa, mu_shift, w0, w_decay_up, a0, w_aaa_up, w_gate_up, k_k, k_a, r_k, ln_x_w, ln_x_b) @ w_proj_a
    yr = rglru_branch(xb, yb, conv_w, conv_b, lru_wa, lru_ba, lru_wx, lru_bx, lru_lambda) @ w_proj_b
    ga, gb = jnp.split(jax.nn.sigmoid(gates), 2, axis=-1)
    h = h + ((ga * ya + gb * yr) @ w_out).astype(h.dtype)
    z = rmsnorm(h, norm_ffn_g) @ w_ff_up
    h = h + (jnp.square(jax.nn.relu(z)) @ w_ff_down).astype(h.dtype)
    return h


def setup_inputs(seed: int = 0) -> dict:
    key = jax.random.key(seed)
    ks = jax.random.split(key, 32)
    f32 = jnp.float32
    nrm = lambda k, shape, s: s * jax.random.normal(k, shape, f32)
    L = DEPTH
    lin = jnp.linspace(0.0, 1.0, WIDTH_A, dtype=f32)
    u_lam = jax.random.uniform(ks[24], (L, WIDTH_B), f32, 0.9, 0.999)
    root = u_lam ** (1.0 / LRU_C)
    return {
        'x': nrm(ks[0], (BATCH, SEQ, D_MODEL), 1.0),
        'meta_tokens': nrm(ks[1], (N_META, D_MODEL), 1.0),
        'norm_mix_g': 1.0 + nrm(ks[2], (L, D_MODEL), 0.02),
        'w_in': nrm(ks[3], (L, D_MODEL, N_IN_COLS), D_MODEL ** -0.5),
        'mu_shift': jax.random.uniform(ks[4], (L, COLS_A), f32),
        'w0': -7.0 + 5.0 * lin ** 0.85 + nrm(ks[5], (L, WIDTH_A), 0.1),
        'w_decay_up': nrm(ks[6], (L, DECAY_LORA, WIDTH_A), 0.1),
        'a0': nrm(ks[7], (L, WIDTH_A), 0.1),
        'w_aaa_up': nrm(ks[8], (L, AAA_LORA, WIDTH_A), 0.5 * AAA_LORA ** -0.5),
        'w_gate_up': nrm(ks[9], (L, GATE_LORA, WIDTH_A), GATE_LORA ** -0.5),
        'k_k': 0.85 + nrm(ks[10], (L, WIDTH_A), 0.05),
        'k_a': 1.0 + nrm(ks[11], (L, WIDTH_A), 0.05),
        'r_k': nrm(ks[12], (L, N_HEADS_A, HEAD_SIZE_A), 0.1),
        'ln_x_w': 1.0 + nrm(ks[13], (L, WIDTH_A), 0.02),
        'ln_x_b': nrm(ks[14], (L, WIDTH_A), 0.02),
        'w_proj_a': nrm(ks[15], (L, WIDTH_A, D_MODEL), WIDTH_A ** -0.5),
        'conv_w': nrm(ks[16], (L, CONV_WIDTH, WIDTH_B), CONV_WIDTH ** -0.5),
        'conv_b': nrm(ks[17], (L, WIDTH_B), 0.02),
        'lru_wa': nrm(ks[18], (L, N_BLOCKS_B, BLOCK_B, BLOCK_B), BLOCK_B ** -0.5),
        'lru_ba': nrm(ks[19], (L, WIDTH_B), 0.02),
        'lru_wx': nrm(ks[20], (L, N_BLOCKS_B, BLOCK_B, BLOCK_B), BLOCK_B ** -0.5),
        'lru_bx': nrm(ks[21], (L, WIDTH_B), 0.02),
        'lru_lambda': jnp.log(root) - jnp.log1p(-root),
        'w_proj_b': nrm(ks[22], (L, WIDTH_B, D_MODEL), WIDTH_B ** -0.5),
        'w_out': nrm(ks[23], (L, D_MODEL, D_MODEL), D_MODEL ** -0.5),
        'norm_ffn_g': 1.0 + nrm(ks[25], (L, D_MODEL), 0.02),
        'w_ff_up': nrm(ks[26], (L, D_MODEL, D_FF), D_MODEL ** -0.5),
        'w_ff_down': nrm(ks[27], (L, D_FF, D_MODEL), D_FF ** -0.5),
        'norm_final_g': 1.0 + nrm(ks[28], (D_MODEL,), 0.02),
    }


def reference(x, meta_tokens, norm_mix_g, w_in, mu_shift, w0, w_decay_up, a0, w_aaa_up, w_gate_up, k_k, k_a,
              r_k, ln_x_w, ln_x_b, w_proj_a, conv_w, conv_b, lru_wa, lru_ba, lru_wx, lru_bx, lru_lambda,
              w_proj_b, w_out, norm_ffn_g, w_ff_up, w_ff_down, norm_final_g):
    B = x.shape[0]
    meta = jnp.broadcast_to(meta_tokens.astype(x.dtype)[None], (B, N_META, D_MODEL))
    h = jnp.concatenate([meta, x], axis=1)
    layer_params = (norm_mix_g, w_in, mu_shift, w0, w_decay_up, a0, w_aaa_up, w_gate_up, k_k, k_a, r_k,
                    ln_x_w, ln_x_b, w_proj_a, conv_w, conv_b, lru_wa, lru_ba, lru_wx, lru_bx, lru_lambda,
                    w_proj_b, w_out, norm_ffn_g, w_ff_up, w_ff_down)
    for l in range(DEPTH):
        h = hybrid_layer(h, *(p[l] for p in layer_params))
    h = rmsnorm(h, norm_final_g)
    return h[:, N_META:]
```

```python
import numpy as np
from contextlib import ExitStack
import concourse.bass as bass
import concourse.mybir as mybir
from concourse.bass_utils import run_bass_kernel_spmd

F32 = mybir.dt.float32
BF16 = mybir.dt.bfloat16
AF = mybir.ActivationFunctionType
ALU = mybir.AluOpType

D = 1024
NMETA = 16
SEQ = 8192
BATCH = 4
TB = 512
CH = 64
NCH = TB // CH
NIN = 7424
DFF = 4096
ENGS = ["tensor", "vector", "scalar", "gpsimd", "sync"]
EDEC = 0.6065306597126334


class Sched:
    def __init__(self):
        self.ops = []

    closed = False

    def ckpt(self, name):
        import os
        if os.environ.get("KSTOP") == name:
            self.closed = True

    def op(self, eng, fn, reads=(), writes=(), dma=False):
        if self.closed:
            return
        self.ops.append(dict(eng=eng, fn=fn, reads=tuple(reads), writes=tuple(writes), dma=dma))

    def plan(self):
        last_w = {}
        readers = {}
        for i, o in enumerate(self.ops):
            deps = set()
            for k in o["reads"]:
                if k in last_w:
                    deps.add(last_w[k])
            for k in o["writes"]:
                if k in last_w:
                    deps.add(last_w[k])
                for r in readers.get(k, ()):
                    deps.add(r)
            deps.discard(i)
            o["deps"] = deps
            for k in o["reads"]:
                readers.setdefault(k, []).append(i)
            for k in o["writes"]:
                last_w[k] = i
                readers[k] = []
        need = set()
        for o in self.ops:
            for d in o["deps"]:
                od = self.ops[d]
                if od["eng"] == "tensor" and o["eng"] == "tensor" and not od["dma"]:
                    continue
                need.add(d)
        cnt = {e: 0 for e in ENGS}
        self.ndma = 8
        dcnt = {}
        drot = {e: 0 for e in ENGS}
        for i, o in enumerate(self.ops):
            o["sig"] = None
            if o["dma"]:
                e = o["eng"]
                slot = drot[e] % self.ndma
                drot[e] += 1
                key = ("dma", e, slot)
                dcnt[key] = dcnt.get(key, 0) + 16
                o["sig"] = (key, dcnt[key])
            elif i in need:
                e = o["eng"]
                cnt[e] += 1
                o["sig"] = (("eng", e), cnt[e])
        waited = {e: {} for e in ENGS}
        for i, o in enumerate(self.ops):
            e = o["eng"]
            w = {}
            if o["dma"]:
                key, val = o["sig"]
                if val > 16:
                    w[key] = val - 16
            for d in o["deps"]:
                od = self.ops[d]
                if od["eng"] == "tensor" and e == "tensor" and not od["dma"]:
                    continue
                key, val = od["sig"]
                if w.get(key, 0) < val:
                    w[key] = val
            w2 = {}
            for key, val in w.items():
                if waited[e].get(key, 0) >= val:
                    continue
                waited[e][key] = val
                w2[key] = val
            o["waits"] = w2
        self.final = {}
        for o in self.ops:
            if o["sig"] is not None:
                key, val = o["sig"]
                self.final[key] = max(self.final.get(key, 0), val)

    def sem_keys(self):
        return sorted({o["sig"][0] for o in self.ops if o["sig"] is not None}, key=str)

    def emit(self, block, sems):
        per = {e: [o for o in self.ops if o["eng"] == e] for e in ENGS}

        def mk(e):
            def body(eng):
                for o in per[e]:
                    for key, val in o["waits"].items():
                        eng.wait_ge(sems[key], val)
                    ins = o["fn"](eng)
                    if o["sig"] is not None:
                        ins.then_inc(sems[o["sig"][0]], 16 if o["dma"] else 1)
                if e == "sync":
                    for key, val in self.final.items():
                        eng.wait_ge(sems[key], val)
            return body
        block.tensor(mk("tensor"))
        block.vector(mk("vector"))
        block.scalar(mk("scalar"))
        block.gpsimd(mk("gpsimd"))
        block.sync(mk("sync"))


COLS = {}
_c = 0
for _n, _w in [("g1", 8), ("mu", 26), ("w0", 8), ("a0", 8), ("kk", 8), ("ka", 8), ("rk", 8), ("lnw", 8),
               ("lnb", 8), ("cw0", 8), ("cw1", 8), ("cw2", 8), ("cw3", 8), ("cb", 8), ("ba", 8), ("bx", 8),
               ("lam", 8), ("g2", 8), ("gf", 8)]:
    COLS[_n] = _c
    _c += _w
NCOLS = _c

ORDER = [24, 25]
for _ct in range(8):
    ORDER += [_ct, 8 + _ct, 16 + _ct]
for _ct in range(8):
    ORDER += [26 + _ct, 34 + _ct]
ORDER += list(range(42, 58))


def build(nblk, debug_taps=False):
    TP = nblk * TB
    nc = bass.Bass("TRN2", target_bir_lowering=False)
    dt_in = lambda n, s: nc.dram_tensor(n, s, F32, kind="ExternalInput").ap()
    xT = dt_in("xT", [D, TP])
    cols_d = dt_in("cols", [128, NCOLS])
    consts_d = dt_in("consts", [128, 128 * 3 + 320 + 512])
    w_in_d = dt_in("w_in", [D, NIN])
    w_pa_d = dt_in("w_pa", [D, D])
    w_pb_d = dt_in("w_pb", [D, D])
    w_o_d = dt_in("w_o", [D, D])
    w_up_d = dt_in("w_up", [D, DFF])
    w_dn_d = dt_in("w_dn", [DFF, D])
    lora_d = dt_in("lora", [128, D])
    gup_d = dt_in("gup", [128, D])
    lwa_d = dt_in("lwa", [16, 64, 64])
    lwx_d = dt_in("lwx", [16, 64, 64])
    outT = nc.dram_tensor("outT", [D, TP], F32, kind="ExternalOutput").ap()
    s_in = nc.dram_tensor("s_in", [D, NIN], BF16).ap()
    s_pa = nc.dram_tensor("s_pa", [D, D], BF16).ap()
    s_pb = nc.dram_tensor("s_pb", [D, D], BF16).ap()
    s_o = nc.dram_tensor("s_o", [D, D], BF16).ap()
    s_up = nc.dram_tensor("s_up", [D, DFF], BF16).ap()
    s_dn = nc.dram_tensor("s_dn", [DFF, D], BF16).ap()

    S = Sched()
    es = ExitStack()
    with es:
        def sb(name, shape, dt=F32):
            return es.enter_context(nc.sbuf_tensor("sb_" + name, shape, dt))

        def ps(name, shape, dt=F32):
            return es.enter_context(nc.psum_tensor("ps_" + name, shape, dt))

        cols = sb("cols", [128, NCOLS])
        dcols = sb("dcols", [128, 26 + 8 + 8 + 8])
        cst = sb("cst", [128, 128 * 3 + 320 + 512])
        identb = sb("identb", [128, 128], BF16)
        blkb = sb("blkb", [128, 128], BF16)
        onesb = sb("onesb", [128, 128], BF16)
        blkf = cst[:, 128:256]
        mask320 = cst[:, 384:704]
        scanmask = cst[:, 704:1216]
        lorab = sb("lorab", [128, D], BF16)
        gupb = sb("gupb", [128, D], BF16)
        wab = sb("wab", [128, D], BF16)
        wxb = sb("wxb", [128, D], BF16)
        NRING = 3
        wring = [sb(f"wring{i}", [128, 8 * 1024], BF16) for i in range(NRING)]
        h = [sb(f"h{i}", [128, TB]) for i in range(8)]
        ub = [sb(f"ub{i}", [128, TB], BF16) for i in range(8)]
        gb = [sb(f"gb{i}", [128, TB], BF16) for i in range(16)]
        ygb = [sb(f"ygb{i}", [128, TB], BF16) for i in range(8)]
        lob = [sb(f"lob{i}", [128, TB], BF16) for i in range(8)]
        NT_ = 16
        T = [sb(f"T{i}", [128, TB]) for i in range(NT_)]
        Bt = [sb(f"Bt{i}", [128, TB], BF16) for i in range(4)]
        pm = [sb(f"pm{i}", [128, TB + 1]) for i in range(2)]
        xbh = sb("xbh", [128, TB + 3])
        rkv = [sb(f"rkv{i}", [128, TB]) for i in range(3)]
        tl = sb("tl", [128, TB], BF16)
        sgd = sb("sgd", [128, TB], BF16)
        AR = sb("AR", [128, NCH, 128], BF16)
        BK = sb("BK", [128, NCH, 128], BF16)
        vb = sb("vb", [128, TB], BF16)
        gam = sb("gam", [128, NCH])
        carry = sb("carry", [128, 26])
        hist = sb("hist", [128, 24])
        lstate = sb("lstate", [128, 8])
        Hf = sb("Hf", [128, 8 * 64])
        Hb = [sb(f"Hb{i}", [128, 8 * 64], BF16) for i in range(2)]
        wst = [sb(f"wst{i}", [128, 1024]) for i in range(2)]
        wsb = [sb(f"wsb{i}", [128, 1024], BF16) for i in range(2)]
        TM = [sb(f"TM{i}", [128, 320], BF16) for i in range(2)]
        PRm = [sb(f"PRm{i}", [128, 320], BF16) for i in range(2)]
        NN = [sb(f"NN{i}", [128, 128], BF16) for i in range(4)]
        XB = [sb(f"XB{i}", [128, 128], BF16) for i in range(2)]
        XF = sb("XF", [128, 128])
        ETb = [sb(f"ETb{i}", [128, 64], BF16) for i in range(2)]
        GG = [sb(f"GG{i}", [128, 64]) for i in range(2)]
        QTb = [sb(f"QTb{i}", [128, 64], BF16) for i in range(2)]
        tch = sb("tch", [128, 64])
        ost = [sb(f"ost{i}", [128, TB]) for i in range(2)]
        PM = [ps(f"PM{i}", [128, TB]) for i in range(2)]
        PS = [ps(f"PS{i}", [128, TB]) for i in range(2)]
        PW = ps("PW", [128, TB])
        PX = ps("PX", [128, TB])
        PY = ps("PY", [128, TB])
        PT = ps("PT", [128, 1024], BF16)

        C = lambda name, i=0: cols[:, COLS[name] + i:COLS[name] + i + 1]

        def dma(eng, out, in_, reads=(), writes=()):
            S.op(eng, lambda e: e.dma_start(out=out, in_=in_), reads=reads, writes=writes, dma=True)

        def acopy(out, in_, reads, writes):
            S.op("scalar", lambda e: e.copy(out=out, in_=in_), reads=reads, writes=writes)

        scr_keys = []

        S.op("sync", lambda e: e.dma_start(out=cols[:], in_=cols_d), writes=["cols"], dma=True)
        S.op("sync", lambda e: e.dma_start(out=cst[:], in_=consts_d), writes=["cst"], dma=True)
        S.op("vector", lambda e: e.tensor_copy(out=identb[:], in_=cst[:, 0:128]), reads=["cst"], writes=["identb"])
        S.op("vector", lambda e: e.tensor_copy(out=blkb[:], in_=cst[:, 128:256]), reads=["cst"], writes=["blkb"])
        S.op("vector", lambda e: e.tensor_copy(out=onesb[:], in_=cst[:, 256:384]), reads=["cst"], writes=["onesb"])
        S.op("vector", lambda e: e.tensor_scalar(out=dcols[:, 0:26], in0=cols[:, COLS["mu"]:COLS["mu"] + 26], scalar1=-1.0, scalar2=1.0, op0=ALU.mult, op1=ALU.add), reads=["cols"], writes=["dcols"])
        S.op("vector", lambda e: e.tensor_scalar(out=dcols[:, 26:34], in0=cols[:, COLS["ka"]:COLS["ka"] + 8], scalar1=-1.0, scalar2=1.0, op0=ALU.mult, op1=ALU.add), reads=["cols"], writes=["dcols"])
        S.op("scalar", lambda e: e.activation(out=dcols[:, 34:42], in_=cols[:, COLS["lam"]:COLS["lam"] + 8], func=AF.Exp, scale=-1.0), reads=["cols", "dcols"], writes=["dcols"])
        S.op("scalar", lambda e: e.activation(out=dcols[:, 34:42], in_=dcols[:, 34:42], func=AF.Ln, bias=1.0, scale=1.0), reads=["dcols"], writes=["dcols"])
        S.op("vector", lambda e: e.tensor_scalar(out=dcols[:, 42:50], in0=dcols[:, 34:42], scalar1=-16.0, scalar2=None, op0=ALU.mult), reads=["dcols"], writes=["dcols"])
        S.op("vector", lambda e: e.tensor_scalar(out=dcols[:, 34:42], in0=dcols[:, 34:42], scalar1=-8.0, scalar2=None, op0=ALU.mult), reads=["dcols"], writes=["dcols"])
        OMU = lambda i: dcols[:, i:i + 1]
        OMKA = lambda i: dcols[:, 26 + i:27 + i]
        CL = lambda i: dcols[:, 34 + i:35 + i]
        CL2 = lambda i: dcols[:, 42 + i:43 + i]
        for t_, k_ in [(carry, "carry"), (hist, "hist"), (lstate, "lstate"), (Hf, "Hf")]:
            S.op("gpsimd", (lambda t_: lambda e: e.memset(t_[:], 0.0))(t_), writes=[k_])
        S.op("gpsimd", lambda e: e.memset(Hb[0][:], 0.0), writes=["Hb0"])
        S.op("gpsimd", lambda e: e.memset(Hb[1][:], 0.0), writes=["Hb1"])
        S.op("sync", lambda e: e.dma_start(out=wst[0][:], in_=lora_d), writes=["wst0"], dma=True)
        S.op("vector", lambda e: e.tensor_copy(out=lorab[:], in_=wst[0][:]), reads=["wst0"], writes=["lorab"])
        S.op("sync", lambda e: e.dma_start(out=wst[1][:], in_=gup_d), writes=["wst1"], dma=True)
        S.op("vector", lambda e: e.tensor_copy(out=gupb[:], in_=wst[1][:]), reads=["wst1"], writes=["gupb"])
        for src_, dst, nm, si in [(lwa_d, wab, "wab", 0), (lwx_d, wxb, "wxb", 1)]:
            S.op("gpsimd", (lambda si: lambda e: e.memset(wst[si][:], 0.0))(si), writes=[f"wst{si}"])
            for hh in range(16):
                ct, par = hh // 2, hh % 2
                dma("sync", wst[si][par * 64:(par + 1) * 64, ct * 128 + par * 64: ct * 128 + par * 64 + 64], src_[hh], writes=[f"wst{si}"])
            cp_dst = dst
            S.op("vector", (lambda dst, si: lambda e: e.tensor_copy(out=dst[:], in_=wst[si][:]))(dst, si), reads=[f"wst{si}"], writes=[nm])

        cast_i = [0]

        def cast_chunk(src_ap, dst_ap, ncol, loads=None):
            i = cast_i[0] % 2
            eng = ["vector", "gpsimd"][(cast_i[0] // 2) % 2]
            cast_i[0] += 1
            if loads is None:
                dma("sync", wst[i][:, 0:ncol], src_ap, writes=[f"wst{i}"])
            else:
                for vf, sa in loads:
                    dma("sync", vf(wst[i]), sa, writes=[f"wst{i}"])
            S.op(eng, (lambda i, ncol: lambda e: e.tensor_copy(out=wsb[i][:, 0:ncol], in_=wst[i][:, 0:ncol]))(i, ncol), reads=[f"wst{i}"], writes=[f"wsb{i}"])
            key = ("scr", len(scr_keys))
            scr_keys.append(key)
            dma("sync", dst_ap, wsb[i][:, 0:ncol], reads=[f"wsb{i}"], writes=[key])

        for kt in range(8):
            rows = slice(kt * 128, (kt + 1) * 128)
            cast_chunk(w_in_d[rows, 3072:3328], s_in[rows, 0:256], 256)
            for c0 in range(0, 8, 2):
                loads = []
                for th in range(3):
                    sa = w_in_d[rows, th * 1024 + c0 * 128: th * 1024 + c0 * 128 + 256].rearrange("p (ct c) -> p ct c", ct=2, c=128)
                    vf = (lambda th: lambda t_: t_[:, 0:768].rearrange("p (ct three c) -> p ct three c", ct=2, three=3, c=128)[:, :, th, :])(th)
                    loads.append((vf, sa))
                cast_chunk(None, s_in[rows, 256 + c0 * 384: 256 + c0 * 384 + 768], 768, loads=loads)
            for c0 in range(0, 8, 4):
                loads = []
                for tw in range(2):
                    sa = w_in_d[rows, 3328 + tw * 1024 + c0 * 128: 3328 + tw * 1024 + c0 * 128 + 512].rearrange("p (ct c) -> p ct c", ct=4, c=128)
                    vf = (lambda tw: lambda t_: t_[:, 0:1024].rearrange("p (ct two c) -> p ct two c", ct=4, two=2, c=128)[:, :, tw, :])(tw)
                    loads.append((vf, sa))
                cast_chunk(None, s_in[rows, 3328 + c0 * 256: 3328 + c0 * 256 + 1024], 1024, loads=loads)
            for c0 in range(5376, 7424, 1024):
                cast_chunk(w_in_d[rows, c0:c0 + 1024], s_in[rows, c0:c0 + 1024], 1024)
            for wd_, sd_ in [(w_pa_d, s_pa), (w_pb_d, s_pb), (w_o_d, s_o)]:
                cast_chunk(wd_[rows, :], sd_[rows, :], 1024)
            for c0 in range(0, DFF, 1024):
                cast_chunk(w_up_d[rows, c0:c0 + 1024], s_up[rows, c0:c0 + 1024], 1024)
        for kt in range(32):
            rows = slice(kt * 128, (kt + 1) * 128)
            cast_chunk(w_dn_d[rows, :], s_dn[rows, :], 1024)

        S.ckpt('setup')
        ring_i = [0]
        cur = {}

        def wload(tag, src_ap, nk, ncol):
            i = ring_i[0] % NRING
            ring_i[0] += 1
            view = wring[i][:, 0:nk * ncol].rearrange("p (k c) -> p k c", k=nk, c=ncol)
            dma("sync", view, src_ap.rearrange("(k p) c -> p k c", p=128), reads=list(scr_keys), writes=[f"wring{i}"])
            return view, f"wring{i}"

        def mm(out, lhsT, rhs, start, stop, reads, writes, tp=None):
            if tp is None:
                S.op("tensor", lambda e: e.matmul(out, lhsT=lhsT, rhs=rhs, start=start, stop=stop), reads=reads, writes=writes)
            else:
                S.op("tensor", lambda e: e.matmul(out, lhsT=lhsT, rhs=rhs, start=start, stop=stop, tile_position=tp), reads=reads, writes=writes)

        def act(out, in_, func, reads, writes, bias=None, scale=None, eng="scalar"):
            kw = {}
            if bias is not None:
                kw["bias"] = bias
            if scale is not None:
                kw["scale"] = scale
            S.op(eng, lambda e: e.activation(out=out, in_=in_, func=func, **kw), reads=reads, writes=writes)

        def tt(out, a, b, op, reads, writes, eng="vector"):
            S.op(eng, lambda e: e.tensor_tensor(out=out, in0=a, in1=b, op=op), reads=reads, writes=writes)

        def stt(out, a, sc, b, op0, op1, reads, writes, eng="vector"):
            S.op(eng, lambda e: e.scalar_tensor_tensor(out=out, in0=a, scalar=sc, in1=b, op0=op0, op1=op1), reads=reads, writes=writes)

        def ts(out, a, s1, s2, op0, op1, reads, writes, eng="vector"):
            if s2 is None:
                S.op(eng, lambda e: e.tensor_scalar(out=out, in0=a, scalar1=s1, scalar2=None, op0=op0), reads=reads, writes=writes)
            else:
                S.op(eng, lambda e: e.tensor_scalar(out=out, in0=a, scalar1=s1, scalar2=s2, op0=op0, op1=op1), reads=reads, writes=writes)

        def cp(out, in_, reads, writes, eng="vector"):
            S.op(eng, lambda e: e.tensor_copy(out=out, in_=in_), reads=reads, writes=writes)

        pm_i = [0]

        def next_pm():
            i = pm_i[0] % 2
            pm_i[0] += 1
            return PM[i], f"PM{i}"

        def rmsnorm_to(dst_fn, gname, eps=1e-6):
            for kt in range(8):
                act(Bt[0][:], h[kt][:], AF.Square, reads=[f"h{kt}"], writes=["Bt0"])
                mm(PS[0][:], onesb[:], Bt[0][:], kt == 0, kt == 7, reads=["onesb", "Bt0"], writes=["PS0"])
            act(T[15][:], PS[0][:], AF.Sqrt, reads=["PS0"], writes=["T15"], bias=eps_col[:, 0:1], scale=1.0 / D)
            S.op("vector", lambda e: e.reciprocal(out=T[15][:], in_=T[15][:]), reads=["T15"], writes=["T15"])
            for kt in range(8):
                o_ap, o_key = dst_fn(kt)
                stt(o_ap, h[kt][:], C(gname, kt), T[15][:], ALU.mult, ALU.mult, reads=[f"h{kt}", "T15", "cols"], writes=[o_key])

        eps_col = sb("eps_col", [128, 2])
        S.op("gpsimd", lambda e: e.memset(eps_col[:, 0:1], 1e-6), writes=["eps_col"])
        S.op("gpsimd", lambda e: e.memset(eps_col[:, 1:2], 64e-5), writes=["eps_col"])

        for blk in range(nblk):
            t0 = blk * TB
            for kt in range(8):
                dma("sync", h[kt][:], xT[kt * 128:(kt + 1) * 128, t0:t0 + TB], writes=[f"h{kt}"])
            rmsnorm_to(lambda kt: (ub[kt][:], f"ub{kt}"), "g1")

            wgrp = {}

            def win_tile(nj):
                g = nj // 8
                if g not in wgrp:
                    nt = min(8, 58 - g * 8)
                    wgrp[g] = wload("win", s_in[:, g * 1024:g * 1024 + nt * 128], 8, nt * 128)
                view, key = wgrp[g]
                off = (nj % 8) * 128
                P, pk = next_pm()
                for kt in range(8):
                    mm(P[:], view[:, kt, off:off + 128], ub[kt][:], kt == 0, kt == 7, reads=[key, f"ub{kt}"], writes=[pk])
                return P, pk

            def shift_evac(nj, out_ap, out_key):
                P, pk = win_tile(nj)
                oj = ORDER[nj]
                b = nj % 2
                act(pm[b][:, 1:TB + 1], P[:], AF.Identity, reads=[pk, "cols"], writes=[f"pm{b}"], scale=C("mu", oj))
                act(pm[b][:, 0:1], carry[:, nj:nj + 1], AF.Identity, reads=["carry", f"pm{b}"], writes=[f"pm{b}"])
                stt(out_ap, P[:], OMU(oj), pm[b][:, 0:TB], ALU.mult, ALU.add, reads=[pk, "dcols", f"pm{b}"], writes=[out_key])
                act(carry[:, nj:nj + 1], pm[b][:, TB:TB + 1], AF.Identity, reads=[f"pm{b}"], writes=["carry"])

            S.ckpt('norm1')
            shift_evac(0, T[0][:], "T0")
            act(tl[0:64, :], T[0][0:64, :], AF.Tanh, reads=["T0"], writes=["tl"])
            cp(tl[64:128, :], T[0][64:128, :], reads=["T0", "tl"], writes=["tl"])
            shift_evac(1, T[0][:], "T0")
            act(sgd[:], T[0][:], AF.Sigmoid, reads=["T0"], writes=["sgd"])

            S.ckpt('lora')
            for ct in range(8):
                cs_ = slice(ct * 128, (ct + 1) * 128)
                for q in range(3):
                    shift_evac(2 + ct * 3 + q, rkv[q][:], f"rkv{q}")
                r_, k_, v_ = rkv[0], rkv[1], rkv[2]
                mm(PS[0][:], lorab[0:64, cs_], tl[0:64, :], True, True, reads=["lorab", "tl"], writes=["PS0"])
                mm(PS[1][:], lorab[64:128, cs_], tl[64:128, :], True, True, reads=["lorab", "tl"], writes=["PS1"], tp=(64, 0))
                act(T[1][:], PS[0][:], AF.Sigmoid, reads=["PS0", "cols"], writes=["T1"], bias=C("w0", ct), scale=1.0)
                act(T[2][:], PS[1][:], AF.Sigmoid, reads=["PS1", "cols"], writes=["T2"], bias=C("a0", ct), scale=1.0)
                S.op("vector", lambda e: e.tensor_tensor_scan(out=T[3][:], data0=scanmask, data1=T[1][:], initial=0.0, op0=ALU.mult, op1=ALU.add), reads=["cst", "T1"], writes=["T3"])
                act(T[4][:], T[3][:], AF.Exp, reads=["T3"], writes=["T4"], scale=-EDEC)
                act(T[5][:], T[3][:], AF.Exp, reads=["T3"], writes=["T5"], scale=EDEC)
                tt(T[6][:], T[3][:], T[1][:], ALU.subtract, reads=["T3", "T1"], writes=["T6"], eng="gpsimd")
                act(T[6][:], T[6][:], AF.Exp, reads=["T6"], writes=["T6"], scale=-EDEC)
                cp(gam[:], T[4][:].rearrange("p (c t) -> p c t", t=CH)[:, :, CH - 1], reads=["T4"], writes=["gam"])
                ts(T[7][:], k_[:], C("kk", ct), None, ALU.mult, None, reads=["rkv1", "cols"], writes=["T7"], eng="gpsimd")
                act(Bt[0][:], T[7][:], AF.Square, reads=["T7"], writes=["Bt0"])
                mm(PS[0][:], blkb[:], Bt[0][:], True, True, reads=["blkb", "Bt0"], writes=["PS0"])
                act(T[8][:], PS[0][:], AF.Sqrt, reads=["PS0"], writes=["T8"])
                ts(T[8][:], T[8][:], 1e-12, None, ALU.max, None, reads=["T8"], writes=["T8"])
                S.op("vector", lambda e: e.reciprocal(out=T[8][:], in_=T[8][:]), reads=["T8"], writes=["T8"])
                tt(T[7][:], T[7][:], T[8][:], ALU.mult, reads=["T7", "T8"], writes=["T7"])
                ts(T[9][:], T[2][:], C("ka", ct), OMKA(ct), ALU.mult, ALU.add, reads=["T2", "cols", "dcols"], writes=["T9"], eng="gpsimd")
                tt(T[9][:], T[9][:], k_[:], ALU.mult, reads=["T9", "rkv1"], writes=["T9"], eng="gpsimd")
                stt(Bt[1][:], r_[:], C("rk", ct), T[9][:], ALU.mult, ALU.mult, reads=["rkv0", "cols", "T9"], writes=["Bt1"])
                mm(PS[1][:], blkb[:], Bt[1][:], True, True, reads=["blkb", "Bt1"], writes=["PS1"])
                tt(T[10][:], PS[1][:], v_[:], ALU.mult, reads=["PS1", "rkv2"], writes=["T10"])
                v3 = lambda t_: t_[:].rearrange("p (c t) -> p c t", t=CH)
                tt(AR[:, :, 64:128], v3(r_), v3(T[4]), ALU.mult, reads=["rkv0", "T4"], writes=["AR"])
                stt(AR[:, :, 0:64], v3(T[7]), -1.0, v3(T[6]), ALU.mult, ALU.mult, reads=["T7", "T6"], writes=["AR"])
                tt(BK[:, :, 64:128], v3(T[9]), v3(T[5]), ALU.mult, reads=["T9", "T5"], writes=["BK"])
                tt(T[8][:], T[7][:], T[2][:], ALU.mult, reads=["T7", "T2"], writes=["T8"], eng="gpsimd")
                tt(BK[:, :, 0:64], v3(T[8]), v3(T[5]), ALU.mult, reads=["T8", "T5"], writes=["BK"])
                act(vb[:], v_[:], AF.Identity, reads=["rkv2"], writes=["vb"])

                S.ckpt('rwkv_elem')
                hs_ = slice(ct * 64, (ct + 1) * 64)
                for c in range(NCH):
                    gi = (blk * 8 + ct) * NCH + c
                    p = gi % 2
                    ccol = slice(c * CH, (c + 1) * CH)
                    ptc = slice(p * 512, p * 512 + 320)
                    srcs = [(AR[:, c, 0:64], "AR", 0), (BK[:, c, 0:64], "BK", 128), (BK[:, c, 64:128], "BK", 192), (vb[:, ccol], "vb", 256)]
                    for (src, sk, off) in srcs:
                        for ln in range(2):
                            lr = slice(ln * 64, (ln + 1) * 64)
                            S.op("tensor", (lambda src, lr, off, ln, p: lambda e: e.transpose(PT[lr, p * 512 + off:p * 512 + off + 64], src[lr], identb[lr, lr], tile_position=(ln * 64, ln * 64)))(src, lr, off, ln, p),
                                 reads=[sk, "identb"], writes=[f"PT{p}"])
                    S.op("scalar", (lambda p: lambda e: e.copy(out=TM[p][:, 0:64], in_=PT[:, p * 512:p * 512 + 64]))(p), reads=[f"PT{p}"], writes=[f"TM{p}"])
                    S.op("scalar", (lambda p: lambda e: e.copy(out=TM[p][:, 128:320], in_=PT[:, p * 512 + 128:p * 512 + 320]))(p), reads=[f"PT{p}"], writes=[f"TM{p}"])
                    S.ckpt('transp')
                    for ln in range(2):
                        lr = slice(ln * 64, (ln + 1) * 64)
                        tp = (ln * 64, ln * 64)
                        mm(PW[lr, 0:128], BK[lr, c, 0:64], AR[lr, c, :], True, True, reads=["BK", "AR"], writes=["PWp"], tp=tp)
                        mm(PW[lr, 128:256], BK[lr, c, 64:128], AR[lr, c, :], True, True, reads=["BK", "AR"], writes=["PWp"], tp=tp)
                        mm(PW[lr, 256:320], AR[lr, c, 0:64], BK[lr, c, 0:64], True, True, reads=["BK", "AR"], writes=["PWp"], tp=tp)
                    tt(PRm[p][:], PW[:, 0:320], mask320, ALU.mult, reads=["PWp", "cst"], writes=[f"PRm{p}"])
                    NTm = PRm[p][:, 0:64]
                    ArbT = PRm[p][:, 64:128]
                    AakT = PRm[p][:, 128:192]
                    ArkT = PRm[p][:, 192:256]
                    Nm = PRm[p][:, 256:320]
                    for ln in range(2):
                        lr = slice(ln * 64, (ln + 1) * 64)
                        mm(PW[lr, 320:384], AakT[lr], TM[p][lr, 256:320], True, True, reads=[f"PRm{p}", f"TM{p}"], writes=["PWa"], tp=(ln * 64, ln * 64))
                    S.op("scalar", (lambda p: lambda e: e.copy(out=TM[p][:, 64:128], in_=PW[:, 320:384]))(p), reads=["PWa"], writes=[f"TM{p}"])
                    S.ckpt('prod')
                    acopy(XF[:, 0:64], PT[:, p * 512:p * 512 + 64], reads=[f"PT{p}"], writes=["XF"])
                    acopy(XF[:, 64:128], PW[:, 320:384], reads=["PWa"], writes=["XF"])
                    Xcur, xkey = TM[p][:, 0:128], f"TM{p}"
                    Ncur, NTcur, nkey = Nm, NTm, f"PRm{p}"
                    for lvl in range(6):
                        xo = (lvl % 2) * 128
                        for ln in range(2):
                            lr = slice(ln * 64, (ln + 1) * 64)
                            mm(PX[lr, 256 + xo:256 + xo + 128], NTcur[lr], Xcur[lr], True, True, reads=[nkey, xkey], writes=[f"PXa{lvl % 2}"], tp=(ln * 64, ln * 64))
                        nx = XB[lvl % 2]
                        tt(XF[:], PX[:, 256 + xo:256 + xo + 128], XF[:], ALU.add, reads=[f"PXa{lvl % 2}", "XF"], writes=["XF"])
                        acopy(nx[:], XF[:], reads=["XF"], writes=[f"XB{lvl % 2}"])
                        Xcur, xkey = nx[:], f"XB{lvl % 2}"
                        if lvl < 5:
                            so = (lvl % 2) * 128
                            for ln in range(2):
                                lr = slice(ln * 64, (ln + 1) * 64)
                                tp = (ln * 64, ln * 64)
                                mm(PX[lr, so:so + 64], NTcur[lr], Ncur[lr], True, True, reads=[nkey], writes=[f"PXs{lvl % 2}"], tp=tp)
                                mm(PX[lr, so + 64:so + 128], Ncur[lr], NTcur[lr], True, True, reads=[nkey], writes=[f"PXs{lvl % 2}"], tp=tp)
                            nn = NN[lvl % 4]
                            acopy(nn[:], PX[:, so:so + 128], reads=[f"PXs{lvl % 2}"], writes=[f"NN{lvl % 4}"])
                            Ncur, NTcur, nkey = nn[:, 0:64], nn[:, 64:128], f"NN{lvl % 4}"
                    Wp = Xcur[:, 0:64]
                    U0 = Xcur[:, 64:128]
                    Btm = TM[p][:, 128:192]
                    Ktm = TM[p][:, 192:256]
                    Vtm = TM[p][:, 256:320]
                    S.ckpt('dbl')
                    for ln in range(2):
                        lr = slice(ln * 64, (ln + 1) * 64)
                        tp = (ln * 64, ln * 64)
                        mm(PW[lr, 384:448], Wp[lr], Btm[lr], True, True, reads=[xkey, f"TM{p}"], writes=["PWe"], tp=tp)
                        mm(PW[lr, 448:512], Btm[lr], U0[lr], True, False, reads=[xkey, f"TM{p}"], writes=["PWg"], tp=tp)
                        mm(PW[lr, 448:512], Ktm[lr], Vtm[lr], False, True, reads=[f"TM{p}"], writes=["PWg"], tp=tp)
                        mm(PS[0][lr, p * 64:p * 64 + 64], Wp[lr], ArbT[lr], True, False, reads=[xkey, f"PRm{p}"], writes=["PS0"], tp=tp)
                        mm(PS[0][lr, p * 64:p * 64 + 64], identb[lr, lr], AR[lr, c, 64:128], False, True, reads=["identb", "AR"], writes=["PS0"], tp=tp)
                    S.op("scalar", (lambda p: lambda e: e.copy(out=ETb[p][:], in_=PW[:, 384:448]))(p), reads=["PWe"], writes=[f"ETb{p}"])
                    act(GG[p][:], PW[:, 448:512], AF.Identity, reads=["PWg", "gam"], writes=[f"GG{p}"], scale=gam[:, c:c + 1])
                    acopy(QTb[p][:], PS[0][:, p * 64:p * 64 + 64], reads=["PS0"], writes=[f"QTb{p}"])
                    hp = gi % 2
                    sp = c % 2
                    for ln in range(2):
                        lr = slice(ln * 64, (ln + 1) * 64)
                        tp = (ln * 64, ln * 64)
                        mm(PY[lr, ccol], U0[lr], ArbT[lr], True, False, reads=[xkey, f"PRm{p}"], writes=["PY"], tp=tp)
                        mm(PY[lr, ccol], Vtm[lr], ArkT[lr], False, False, reads=[f"TM{p}", f"PRm{p}"], writes=["PY"], tp=tp)
                        mm(PY[lr, ccol], Hb[sp][lr, hs_], QTb[p][lr], False, True, reads=[f"Hb{sp}", f"QTb{p}"], writes=["PY"], tp=tp)
                    for ln in range(2):
                        lr = slice(ln * 64, (ln + 1) * 64)
                        mm(PS[1][lr, 0:64], ETb[p][lr], Hb[sp][lr, hs_], True, True, reads=[f"ETb{p}", f"Hb{sp}"], writes=["PS1"], tp=(ln * 64, ln * 64))
                    tt(tch[:], PS[1][:, 0:64], Hf[:, hs_], ALU.add, reads=["PS1", "Hf"], writes=["tch"])
                    stt(Hf[:, hs_], tch[:], gam[:, c:c + 1], GG[p][:], ALU.mult, ALU.add, reads=["tch", "gam", f"GG{p}"], writes=["Hf"])
                    acopy(Hb[1 - sp][:, hs_], Hf[:, hs_], reads=["Hf"], writes=[f"Hb{1 - sp}"])
                S.ckpt('wkv')
                act(T[11][:], PY[:], AF.Identity, reads=["PY"], writes=["T11"])
                act(T[12][:], PY[:], AF.Square, reads=["PY"], writes=["T12"])
                mm(PS[0][:], blkf, T[11][:], True, True, reads=["cst", "T11"], writes=["PS0"])
                mm(PS[1][:], blkf, T[12][:], True, True, reads=["cst", "T12"], writes=["PS1"])
                ts(T[13][:], PS[0][:], 1.0 / 64, None, ALU.mult, None, reads=["PS0"], writes=["T13"])
                tt(T[14][:], T[13][:], T[13][:], ALU.mult, reads=["T13"], writes=["T14"], eng="gpsimd")
                stt(T[12][:], PS[1][:], 1.0 / 64, T[14][:], ALU.mult, ALU.subtract, reads=["PS1", "T14"], writes=["T12"])
                act(T[12][:], T[12][:], AF.Sqrt, reads=["T12"], writes=["T12"], bias=eps_col[:, 1:2], scale=1.0)
                S.op("vector", lambda e: e.reciprocal(out=T[12][:], in_=T[12][:]), reads=["T12"], writes=["T12"])
                tt(T[11][:], T[11][:], T[13][:], ALU.subtract, reads=["T11", "T13"], writes=["T11"])
                tt(T[11][:], T[11][:], T[12][:], ALU.mult, reads=["T11", "T12"], writes=["T11"])
                act(T[11][:], T[11][:], AF.Identity, reads=["T11", "cols"], writes=["T11"], bias=C("lnb", ct), scale=C("lnw", ct))
                tt(T[11][:], T[11][:], T[10][:], ALU.add, reads=["T11", "T10"], writes=["T11"], eng="gpsimd")
                mm(PS[0][:], gupb[:, cs_], sgd[:], True, True, reads=["gupb", "sgd"], writes=["PS0"])
                tt(ygb[ct][:], PS[0][:], T[11][:], ALU.mult, reads=["PS0", "T11"], writes=[f"ygb{ct}"])

            S.ckpt('rwkv')
            for ct in range(8):
                cs_ = slice(ct * 128, (ct + 1) * 128)
                P, pk = win_tile(26 + ct * 2)
                act(xbh[:, 3:TB + 3], P[:], AF.Identity, reads=[pk], writes=["xbh"])
                cp(xbh[:, 0:3], hist[:, ct * 3:ct * 3 + 3], reads=["hist", "xbh"], writes=["xbh"], eng="gpsimd")
                cp(hist[:, ct * 3:ct * 3 + 3], xbh[:, TB:TB + 3], reads=["xbh"], writes=["hist"], eng="gpsimd")
                P2, pk2 = win_tile(26 + ct * 2 + 1)
                act(T[0][:], P2[:], AF.Gelu, reads=[pk2], writes=["T0"])
                act(T[1][:], xbh[:, 3:TB + 3], AF.Identity, reads=["xbh", "cols"], writes=["T1"], bias=C("cb", ct), scale=C("cw3", ct))
                for j in range(3):
                    stt(T[1][:], xbh[:, j:j + TB], C(f"cw{j}", ct), T[1][:], ALU.mult, ALU.add, reads=["xbh", "cols", "T1"], writes=["T1"])
                act(Bt[2][:], T[1][:], AF.Identity, reads=["T1"], writes=["Bt2"])
                mm(PS[0][:], wab[:, cs_], Bt[2][:], True, True, reads=["wab", "Bt2"], writes=["PS0"])
                mm(PS[1][:], wxb[:, cs_], Bt[2][:], True, True, reads=["wxb", "Bt2"], writes=["PS1"])
                act(T[2][:], PS[0][:], AF.Sigmoid, reads=["PS0", "cols"], writes=["T2"], bias=C("ba", ct), scale=1.0)
                act(T[3][:], PS[1][:], AF.Sigmoid, reads=["PS1", "cols"], writes=["T3"], bias=C("bx", ct), scale=1.0)
                act(T[4][:], T[2][:], AF.Exp, reads=["T2", "dcols"], writes=["T4"], scale=CL(ct))
                act(T[5][:], T[2][:], AF.Exp, reads=["T2", "dcols"], writes=["T5"], scale=CL2(ct))
                ts(T[5][:], T[5][:], -1.0, 1.0, ALU.mult, ALU.add, reads=["T5"], writes=["T5"], eng="gpsimd")
                act(T[5][:], T[5][:], AF.Sqrt, reads=["T5"], writes=["T5"])
                tt(T[3][:], T[3][:], T[1][:], ALU.mult, reads=["T3", "T1"], writes=["T3"], eng="gpsimd")
                tt(T[3][:], T[3][:], T[5][:], ALU.mult, reads=["T3", "T5"], writes=["T3"])
                S.op("vector", (lambda ct: lambda e: e.tensor_tensor_scan(out=T[6][:], data0=T[4][:], data1=T[3][:], initial=lstate[:, ct:ct + 1], op0=ALU.mult, op1=ALU.add))(ct),
                     reads=["T4", "T3", "lstate"], writes=["T6"])
                cp(lstate[:, ct:ct + 1], T[6][:, TB - 1:TB], reads=["T6"], writes=["lstate"])
                tt(lob[ct][:], T[6][:], T[0][:], ALU.mult, reads=["T6", "T0"], writes=[f"lob{ct}"])

            S.ckpt('lru')
            for g in range(16):
                P, pk = win_tile(42 + g)
                act(gb[g][:], P[:], AF.Sigmoid, reads=[pk], writes=[f"gb{g}"])
            wa_v, wa_k = wload("pa", s_pa, 8, 1024)
            wb_v, wb_k = wload("pb", s_pb, 8, 1024)
            for ot in range(8):
                os_ = slice(ot * 128, (ot + 1) * 128)
                Pa, pka = next_pm()
                for kt in range(8):
                    mm(Pa[:], wa_v[:, kt, os_], ygb[kt][:], kt == 0, kt == 7, reads=[wa_k, f"ygb{kt}"], writes=[pka])
                Pb, pkb = next_pm()
                for kt in range(8):
                    mm(Pb[:], wb_v[:, kt, os_], lob[kt][:], kt == 0, kt == 7, reads=[wb_k, f"lob{kt}"], writes=[pkb])
                acopy(Bt[2][:], Pa[:], reads=[pka], writes=["Bt2"])
                acopy(Bt[3][:], Pb[:], reads=[pkb], writes=["Bt3"])
                tt(T[0][:], Bt[2][:], gb[ot][:], ALU.mult, reads=["Bt2", f"gb{ot}"], writes=["T0"])
                tt(T[1][:], Bt[3][:], gb[8 + ot][:], ALU.mult, reads=["Bt3", f"gb{8 + ot}"], writes=["T1"])
                tt(ub[ot][:], T[0][:], T[1][:], ALU.add, reads=["T0", "T1"], writes=[f"ub{ot}"], eng="gpsimd")
            wo_v, wo_k = wload("po", s_o, 8, 1024)
            for ot in range(8):
                os_ = slice(ot * 128, (ot + 1) * 128)
                P, pk = next_pm()
                for kt in range(8):
                    mm(P[:], wo_v[:, kt, os_], ub[kt][:], kt == 0, kt == 7, reads=[wo_k, f"ub{kt}"], writes=[pk])
                tt(h[ot][:], P[:], h[ot][:], ALU.add, reads=[pk, f"h{ot}"], writes=[f"h{ot}"])
            S.ckpt('mix')
            rmsnorm_to(lambda kt: (ub[kt][:], f"ub{kt}"), "g2")
            for half in range(2):
                for q in range(2):
                    wu_v, wu_k = wload("up", s_up[:, (half * 2 + q) * 1024:(half * 2 + q + 1) * 1024], 8, 1024)
                    for f8 in range(8):
                        fs = slice(f8 * 128, (f8 + 1) * 128)
                        P, pk = next_pm()
                        for kt in range(8):
                            mm(P[:], wu_v[:, kt, fs], ub[kt][:], kt == 0, kt == 7, reads=[wu_k, f"ub{kt}"], writes=[pk])
                        ti = (q * 8 + f8) % 2
                        act(T[ti][:], P[:], AF.Relu, reads=[pk], writes=[f"T{ti}"])
                        tt(gb[q * 8 + f8][:], T[ti][:], T[ti][:], ALU.mult, reads=[f"T{ti}"], writes=[f"gb{q * 8 + f8}"], eng="gpsimd")
                for oh in range(2):
                    wd_v, wd_k = wload("dn", s_dn[half * 2048:(half + 1) * 2048, oh * 512:(oh + 1) * 512], 16, 512)
                    for o4 in range(4):
                        ot = oh * 4 + o4
                        P, pk = next_pm()
                        for ft in range(16):
                            mm(P[:], wd_v[:, ft, o4 * 128:(o4 + 1) * 128], gb[ft][:], ft == 0, ft == 15, reads=[wd_k, f"gb{ft}"], writes=[pk])
                        tt(h[ot][:], P[:], h[ot][:], ALU.add, reads=[pk, f"h{ot}"], writes=[f"h{ot}"])
            oi = [0]

            def fin_dst(kt):
                i = kt % 2
                return ost[i][:], f"ost{i}"
            for kt in range(8):
                act(Bt[0][:], h[kt][:], AF.Square, reads=[f"h{kt}"], writes=["Bt0"])
                mm(PS[0][:], onesb[:], Bt[0][:], kt == 0, kt == 7, reads=["onesb", "Bt0"], writes=["PS0"])
            act(T[15][:], PS[0][:], AF.Sqrt, reads=["PS0"], writes=["T15"], bias=eps_col[:, 0:1], scale=1.0 / D)
            S.op("vector", lambda e: e.reciprocal(out=T[15][:], in_=T[15][:]), reads=["T15"], writes=["T15"])
            for kt in range(8):
                i = kt % 2
                stt(ost[i][:], h[kt][:], C("gf", kt), T[15][:], ALU.mult, ALU.mult, reads=[f"h{kt}", "T15", "cols"], writes=[f"ost{i}"])
                dma("gpsimd", outT[kt * 128:(kt + 1) * 128, t0:t0 + TB], ost[i][:], reads=[f"ost{i}"])

        S.plan()
        sems = {k: es.enter_context(nc.semaphore("s_" + "_".join(map(str, k)))) for k in S.sem_keys()}
        with nc.Block() as block:
            S.emit(block, sems)
    return nc


def _colpack(v, n):
    v = np.asarray(v, np.float32).reshape(n, 128)
    return np.ascontiguousarray(v.T)


def host_prep(inputs, nblk):
    TP = nblk * TB
    L = 0
    cols = np.zeros((128, NCOLS), np.float32)

    def put(name, v, n):
        cols[:, COLS[name]:COLS[name] + n] = _colpack(v, n)
    put("g1", inputs["norm_mix_g"][L], 8)
    put("mu", inputs["mu_shift"][L], 26)
    put("w0", inputs["w0"][L], 8)
    put("a0", inputs["a0"][L], 8)
    put("kk", inputs["k_k"][L], 8)
    put("ka", inputs["k_a"][L], 8)
    put("rk", inputs["r_k"][L].reshape(-1), 8)
    put("lnw", inputs["ln_x_w"][L], 8)
    put("lnb", inputs["ln_x_b"][L], 8)
    for j in range(4):
        put(f"cw{j}", inputs["conv_w"][L][j], 8)
    put("cb", inputs["conv_b"][L], 8)
    put("ba", inputs["lru_ba"][L], 8)
    put("bx", inputs["lru_bx"][L], 8)
    put("lam", inputs["lru_lambda"][L], 8)
    put("g2", inputs["norm_ffn_g"][L], 8)
    put("gf", inputs["norm_final_g"], 8)
    ident = np.eye(128, dtype=np.float32)
    blk = np.kron(np.eye(2, dtype=np.float32), np.ones((64, 64), np.float32))
    ones = np.ones((128, 128), np.float32)
    s = np.arange(64)[:, None]
    t = np.arange(64)[None, :]
    strict = (s < t).astype(np.float32)
    incl = (s <= t).astype(np.float32)
    strictT = (s > t).astype(np.float32)
    m1 = np.concatenate([strict, incl, strict, incl, strictT], axis=1)
    mask320 = np.concatenate([m1, m1], axis=0)
    scanmask = np.ones((128, TB), np.float32)
    scanmask[:, ::CH] = 0.0
    consts = np.ascontiguousarray(np.concatenate([ident, blk, ones, mask320, scanmask], axis=1))
    lora = np.ascontiguousarray(np.concatenate([inputs["w_decay_up"][L], inputs["w_aaa_up"][L]], axis=0).astype(np.float32))
    shared = dict(
        cols=cols, consts=consts,
        w_in=np.ascontiguousarray(inputs["w_in"][L], dtype=np.float32),
        w_pa=np.ascontiguousarray(inputs["w_proj_a"][L], dtype=np.float32),
        w_pb=np.ascontiguousarray(inputs["w_proj_b"][L], dtype=np.float32),
        w_o=np.ascontiguousarray(inputs["w_out"][L], dtype=np.float32),
        w_up=np.ascontiguousarray(inputs["w_ff_up"][L], dtype=np.float32),
        w_dn=np.ascontiguousarray(inputs["w_ff_down"][L], dtype=np.float32),
        lora=lora,
        gup=np.ascontiguousarray(inputs["w_gate_up"][L], dtype=np.float32),
        lwa=np.ascontiguousarray(inputs["lru_wa"][L], dtype=np.float32),
        lwx=np.ascontiguousarray(inputs["lru_wx"][L], dtype=np.float32),
    )
    x = np.asarray(inputs["x"], np.float32)
    meta = np.asarray(inputs["meta_tokens"], np.float32)
    B, Tx, _ = x.shape
    xts = []
    for b in range(B):
        hfull = np.zeros((TP, D), np.float32)
        n = min(TP, NMETA + Tx)
        hfull[:NMETA] = meta
        hfull[NMETA:n] = x[b, :n - NMETA]
        xts.append(np.ascontiguousarray(hfull.T))
    return shared, xts


_NC_CACHE = {}


def run(inputs, nblk, ncores=8):
    shared, xts = host_prep(inputs, nblk)
    if nblk not in _NC_CACHE:
        _NC_CACHE[nblk] = build(nblk)
    nc = _NC_CACHE[nblk]
    B = len(xts)
    in_maps = [dict(shared, xT=xts[i % B]) for i in range(ncores)]
    res = run_bass_kernel_spmd(nc, in_maps, core_ids=list(range(ncores)))
    outs = [np.ascontiguousarray(res.results[b]["outT"].T) for b in range(B)]
    return np.stack(outs, axis=0)


def kernel(**inputs):
    nblk = (NMETA + SEQ + TB - 1) // TB
    full = run(inputs, nblk)
    return np.ascontiguousarray(full[:, NMETA:NMETA + SEQ, :]).astype(np.float32)
```

```python
import numpy as np
from contextlib import ExitStack
import concourse.bass as bass
import concourse.mybir as mybir
from concourse.bass_utils import run_bass_kernel_spmd

F32 = mybir.dt.float32
BF16 = mybir.dt.bfloat16
AF = mybir.ActivationFunctionType
ALU = mybir.AluOpType

D = 1024
NMETA = 16
SEQ = 8192
BATCH = 4
TB = 512
CH = 64
NCH = TB // CH
NIN = 7424
DFF = 4096
ENGS = ["tensor", "vector", "scalar", "gpsimd", "sync"]
EDEC = 0.6065306597126334


class Sched:
    def __init__(self):
        self.ops = []

    closed = False

    def ckpt(self, name):
        import os
        if os.environ.get("KSTOP") == name:
            self.closed = True

    def op(self, eng, fn, reads=(), writes=(), dma=False):
        if self.closed:
            return
        al = {"wst0": ("ost0", "ost1"), "wst1": ("pm0", "pm1")}
        ex = lambda ks: tuple(k2 for k in ks for k2 in al.get(k, (k,)))
        self.ops.append(dict(eng=eng, fn=fn, reads=ex(reads), writes=ex(writes), dma=dma))

    def plan(self):
        last_w = {}
        readers = {}
        for i, o in enumerate(self.ops):
            deps = set()
            for k in o["reads"]:
                if k in last_w:
                    deps.add(last_w[k])
            for k in o["writes"]:
                if k in last_w:
                    deps.add(last_w[k])
                for r in readers.get(k, ()):
                    deps.add(r)
            deps.discard(i)
            o["deps"] = deps
            for k in o["reads"]:
                readers.setdefault(k, []).append(i)
            for k in o["writes"]:
                last_w[k] = i
                readers[k] = []
        need = set()
        for o in self.ops:
            for d in o["deps"]:
                od = self.ops[d]
                if od["eng"] == "tensor" and o["eng"] == "tensor" and not od["dma"]:
                    continue
                need.add(d)
        cnt = {e: 0 for e in ENGS}
        self.ndma = 8
        dcnt = {}
        drot = {e: 0 for e in ENGS}
        for i, o in enumerate(self.ops):
            o["sig"] = None
            if o["dma"]:
                e = o["eng"]
                slot = drot[e] % self.ndma
                drot[e] += 1
                key = ("dma", e, slot)
                dcnt[key] = dcnt.get(key, 0) + 16
                o["sig"] = (key, dcnt[key])
            elif i in need:
                e = o["eng"]
                cnt[e] += 1
                o["sig"] = (("eng", e), cnt[e])
        waited = {e: {} for e in ENGS}
        for i, o in enumerate(self.ops):
            e = o["eng"]
            w = {}
            if o["dma"]:
                key, val = o["sig"]
                if val > 16:
                    w[key] = val - 16
            for d in o["deps"]:
                od = self.ops[d]
                if od["eng"] == "tensor" and e == "tensor" and not od["dma"]:
                    continue
                key, val = od["sig"]
                if w.get(key, 0) < val:
                    w[key] = val
            w2 = {}
            for key, val in w.items():
                if waited[e].get(key, 0) >= val:
                    continue
                waited[e][key] = val
                w2[key] = val
            o["waits"] = w2
        self.final = {}
        for o in self.ops:
            if o["sig"] is not None:
                key, val = o["sig"]
                self.final[key] = max(self.final.get(key, 0), val)

    def sem_keys(self):
        return sorted({o["sig"][0] for o in self.ops if o["sig"] is not None}, key=str)

    def emit(self, block, sems):
        per = {e: [o for o in self.ops if o["eng"] == e] for e in ENGS}

        def mk(e):
            def body(eng):
                for o in per[e]:
                    for key, val in o["waits"].items():
                        eng.wait_ge(sems[key], val)
                    ins = o["fn"](eng)
                    if o["sig"] is not None:
                        ins.then_inc(sems[o["sig"][0]], 16 if o["dma"] else 1)
                if e == "sync":
                    for key, val in self.final.items():
                        eng.wait_ge(sems[key], val)
            return body
        block.tensor(mk("tensor"))
        block.vector(mk("vector"))
        block.scalar(mk("scalar"))
        block.gpsimd(mk("gpsimd"))
        block.sync(mk("sync"))


COLS = {}
_c = 0
for _n, _w in [("g1", 8), ("mu", 26), ("w0", 8), ("a0", 8), ("kk", 8), ("ka", 8), ("rk", 8), ("lnw", 8),
               ("lnb", 8), ("cw0", 8), ("cw1", 8), ("cw2", 8), ("cw3", 8), ("cb", 8), ("ba", 8), ("bx", 8),
               ("lam", 8), ("g2", 8), ("gf", 8)]:
    COLS[_n] = _c
    _c += _w
NCOLS = _c

ORDER = [24, 25]
for _ct in range(8):
    ORDER += [_ct, 8 + _ct, 16 + _ct]
for _ct in range(8):
    ORDER += [26 + _ct, 34 + _ct]
ORDER += list(range(42, 58))


def build(nblk, debug_taps=False):
    TP = nblk * TB
    nc = bass.Bass("TRN2", target_bir_lowering=False)
    dt_in = lambda n, s: nc.dram_tensor(n, s, F32, kind="ExternalInput").ap()
    xT = dt_in("xT", [D, TP])
    cols_d = dt_in("cols", [128, NCOLS])
    consts_d = dt_in("consts", [128, 128 * 3 + 320 + 512])
    w_in_d = dt_in("w_in", [D, NIN])
    w_pa_d = dt_in("w_pa", [D, D])
    w_pb_d = dt_in("w_pb", [D, D])
    w_o_d = dt_in("w_o", [D, D])
    w_up_d = dt_in("w_up", [D, DFF])
    w_dn_d = dt_in("w_dn", [DFF, D])
    lora_d = dt_in("lora", [128, D])
    gup_d = dt_in("gup", [128, D])
    lwa_d = dt_in("lwa", [16, 64, 64])
    lwx_d = dt_in("lwx", [16, 64, 64])
    outT = nc.dram_tensor("outT", [D, TP], F32, kind="ExternalOutput").ap()
    s_in = nc.dram_tensor("s_in", [D, NIN], BF16).ap()
    s_pa = nc.dram_tensor("s_pa", [D, D], BF16).ap()
    s_pb = nc.dram_tensor("s_pb", [D, D], BF16).ap()
    s_o = nc.dram_tensor("s_o", [D, D], BF16).ap()
    s_up = nc.dram_tensor("s_up", [D, DFF], BF16).ap()
    s_dn = nc.dram_tensor("s_dn", [DFF, D], BF16).ap()

    S = Sched()
    es = ExitStack()
    with es:
        def sb(name, shape, dt=F32):
            return es.enter_context(nc.sbuf_tensor("sb_" + name, shape, dt))

        def ps(name, shape, dt=F32):
            return es.enter_context(nc.psum_tensor("ps_" + name, shape, dt))

        cols = sb("cols", [128, NCOLS])
        dcols = sb("dcols", [128, 26 + 8 + 8 + 8])
        cst = sb("cst", [128, 128 * 3 + 320 + 512])
        identb = sb("identb", [128, 128], BF16)
        blkb = sb("blkb", [128, 128], BF16)
        onesb = sb("onesb", [128, 128], BF16)
        blkf = cst[:, 128:256]
        mask320 = cst[:, 384:704]
        scanmask = cst[:, 704:1216]
        lorab = sb("lorab", [128, D], BF16)
        gupb = sb("gupb", [128, D], BF16)
        wab = sb("wab", [128, D], BF16)
        wxb = sb("wxb", [128, D], BF16)
        NRING = 5
        wring = [sb(f"wring{i}", [128, 8 * 512], BF16) for i in range(NRING)]
        h = [sb(f"h{i}", [128, TB]) for i in range(8)]
        ub = [sb(f"ub{i}", [128, TB], BF16) for i in range(8)]
        gb = [sb(f"gb{i}", [128, TB], BF16) for i in range(16)]
        ygb = [sb(f"ygb{i}", [128, TB], BF16) for i in range(8)]
        lob = [sb(f"lob{i}", [128, TB], BF16) for i in range(8)]
        NT_ = 16
        T = [sb(f"T{i}", [128, TB]) if i != 10 else None for i in range(NT_)]
        Bt = [sb(f"Bt{i}", [128, TB], BF16) for i in range(4)]
        big = sb("big", [128, 2052])
        pm = [big[:, 1024:1024 + TB + 1], big[:, 1024 + TB + 1:1024 + 2 * TB + 2]]
        xbh = sb("xbh", [128, TB + 3])
        rkv = [sb(f"rkv{i}", [128, TB]) for i in range(3)]
        tl = sb("tl", [128, TB], BF16)
        sgd = sb("sgd", [128, TB], BF16)
        AR = [sb(f"AR{q}", [128, NCH, 128], BF16) for q in range(2)]
        BK = [sb(f"BK{q}", [128, NCH, 128], BF16) for q in range(2)]
        vb = [sb(f"vb{q}", [128, TB], BF16) for q in range(2)]
        gam = [sb(f"gam{q}", [128, NCH]) for q in range(2)]
        bon = [sb(f"bon{q}", [128, TB]) for q in range(2)]
        carry = sb("carry", [128, 26])
        hist = sb("hist", [128, 24])
        lstate = sb("lstate", [128, 8])
        Hf = sb("Hf", [128, 8 * 64])
        Hb = [sb(f"Hb{i}", [128, 8 * 64], BF16) for i in range(2)]
        wst = [big[:, 0:1024], big[:, 1024:2048]]
        wsb = [wring[i][:, 0:1024] for i in range(2)]
        NS = 8
        TM = [sb(f"TM{j}", [128, 320], BF16) for j in range(NS)]
        PRm = [sb(f"PRm{j}", [128, 320], BF16) for j in range(NS)]
        NN = [[sb(f"NN{j}_{i}", [128, 128], BF16) for i in range(2)] for j in range(4)]
        XB = [[sb(f"XB{j}_{i}", [128, 128], BF16) for i in range(2)] for j in range(NS)]
        XF = [sb(f"XF{j}", [128, 128]) for j in range(4)]
        ETb = [sb(f"ETb{j}", [128, 64], BF16) for j in range(NS)]
        GG = [sb(f"GG{j}", [128, 64]) for j in range(NS)]
        QTb = [sb(f"QTb{j}", [128, 64], BF16) for j in range(NS)]
        tch = sb("tch", [128, 64])
        ost = [big[:, 0:TB], big[:, TB:2 * TB]]
        P3 = [ps(f"PM{i}", [128, TB]) for i in range(3)]
        WA = ps("WA", [128, TB])
        WB = ps("WB", [128, TB])
        WC = ps("WC", [128, TB])
        PY = ps("PY", [128, TB])
        PT = ps("PT", [128, 1024], BF16)

        C = lambda name, i=0: cols[:, COLS[name] + i:COLS[name] + i + 1]

        def dma(eng, out, in_, reads=(), writes=()):
            S.op(eng, lambda e: e.dma_start(out=out, in_=in_), reads=reads, writes=writes, dma=True)

        def acopy(out, in_, reads, writes):
            S.op("scalar", lambda e: e.copy(out=out, in_=in_), reads=reads, writes=writes)

        scr_keys = []

        S.op("sync", lambda e: e.dma_start(out=cols[:], in_=cols_d), writes=["cols"], dma=True)
        S.op("sync", lambda e: e.dma_start(out=cst[:], in_=consts_d), writes=["cst"], dma=True)
        S.op("vector", lambda e: e.tensor_copy(out=identb[:], in_=cst[:, 0:128]), reads=["cst"], writes=["identb"])
        S.op("vector", lambda e: e.tensor_copy(out=blkb[:], in_=cst[:, 128:256]), reads=["cst"], writes=["blkb"])
        S.op("vector", lambda e: e.tensor_copy(out=onesb[:], in_=cst[:, 256:384]), reads=["cst"], writes=["onesb"])
        S.op("vector", lambda e: e.tensor_scalar(out=dcols[:, 0:26], in0=cols[:, COLS["mu"]:COLS["mu"] + 26], scalar1=-1.0, scalar2=1.0, op0=ALU.mult, op1=ALU.add), reads=["cols"], writes=["dcols"])
        S.op("vector", lambda e: e.tensor_scalar(out=dcols[:, 26:34], in0=cols[:, COLS["ka"]:COLS["ka"] + 8], scalar1=-1.0, scalar2=1.0, op0=ALU.mult, op1=ALU.add), reads=["cols"], writes=["dcols"])
        S.op("scalar", lambda e: e.activation(out=dcols[:, 34:42], in_=cols[:, COLS["lam"]:COLS["lam"] + 8], func=AF.Exp, scale=-1.0), reads=["cols", "dcols"], writes=["dcols"])
        S.op("scalar", lambda e: e.activation(out=dcols[:, 34:42], in_=dcols[:, 34:42], func=AF.Ln, bias=1.0, scale=1.0), reads=["dcols"], writes=["dcols"])
        S.op("vector", lambda e: e.tensor_scalar(out=dcols[:, 42:50], in0=dcols[:, 34:42], scalar1=-16.0, scalar2=None, op0=ALU.mult), reads=["dcols"], writes=["dcols"])
        S.op("vector", lambda e: e.tensor_scalar(out=dcols[:, 34:42], in0=dcols[:, 34:42], scalar1=-8.0, scalar2=None, op0=ALU.mult), reads=["dcols"], writes=["dcols"])
        OMU = lambda i: dcols[:, i:i + 1]
        OMKA = lambda i: dcols[:, 26 + i:27 + i]
        CL = lambda i: dcols[:, 34 + i:35 + i]
        CL2 = lambda i: dcols[:, 42 + i:43 + i]
        for t_, k_ in [(carry, "carry"), (hist, "hist"), (lstate, "lstate"), (Hf, "Hf")]:
            S.op("gpsimd", (lambda t_: lambda e: e.memset(t_[:], 0.0))(t_), writes=[k_])
        S.op("gpsimd", lambda e: e.memset(Hb[0][:], 0.0), writes=["Hb0"])
        S.op("gpsimd", lambda e: e.memset(Hb[1][:], 0.0), writes=["Hb1"])
        S.op("sync", lambda e: e.dma_start(out=wst[0][:], in_=lora_d), writes=["wst0"], dma=True)
        S.op("vector", lambda e: e.tensor_copy(out=lorab[:], in_=wst[0][:]), reads=["wst0"], writes=["lorab"])
        S.op("sync", lambda e: e.dma_start(out=wst[1][:], in_=gup_d), writes=["wst1"], dma=True)
        S.op("vector", lambda e: e.tensor_copy(out=gupb[:], in_=wst[1][:]), reads=["wst1"], writes=["gupb"])
        for src_, dst, nm, si in [(lwa_d, wab, "wab", 0), (lwx_d, wxb, "wxb", 1)]:
            S.op("gpsimd", (lambda si: lambda e: e.memset(wst[si][:], 0.0))(si), writes=[f"wst{si}"])
            for hh in range(16):
                ct, par = hh // 2, hh % 2
                dma("sync", wst[si][par * 64:(par + 1) * 64, ct * 128 + par * 64: ct * 128 + par * 64 + 64], src_[hh], writes=[f"wst{si}"])
            cp_dst = dst
            S.op("vector", (lambda dst, si: lambda e: e.tensor_copy(out=dst[:], in_=wst[si][:]))(dst, si), reads=[f"wst{si}"], writes=[nm])

        cast_i = [0]

        def cast_chunk(src_ap, dst_ap, ncol, loads=None):
            i = cast_i[0] % 2
            eng = ["vector", "gpsimd"][(cast_i[0] // 2) % 2]
            cast_i[0] += 1
            if loads is None:
                dma("sync", wst[i][:, 0:ncol], src_ap, writes=[f"wst{i}"])
            else:
                for vf, sa in loads:
                    dma("sync", vf(wst[i]), sa, writes=[f"wst{i}"])
            S.op(eng, (lambda i, ncol: lambda e: e.tensor_copy(out=wsb[i][:, 0:ncol], in_=wst[i][:, 0:ncol]))(i, ncol), reads=[f"wst{i}"], writes=[f"wring{i}"])
            key = ("scr", len(scr_keys))
            scr_keys.append(key)
            dma("sync", dst_ap, wsb[i][:, 0:ncol], reads=[f"wring{i}"], writes=[key])

        for kt in range(8):
            rows = slice(kt * 128, (kt + 1) * 128)
            cast_chunk(w_in_d[rows, 3072:3328], s_in[rows, 0:256], 256)
            for c0 in range(0, 8, 2):
                loads = []
                for th in range(3):
                    sa = w_in_d[rows, th * 1024 + c0 * 128: th * 1024 + c0 * 128 + 256].rearrange("p (ct c) -> p ct c", ct=2, c=128)
                    vf = (lambda th: lambda t_: t_[:, 0:768].rearrange("p (ct three c) -> p ct three c", ct=2, three=3, c=128)[:, :, th, :])(th)
                    loads.append((vf, sa))
                cast_chunk(None, s_in[rows, 256 + c0 * 384: 256 + c0 * 384 + 768], 768, loads=loads)
            for c0 in range(0, 8, 4):
                loads = []
                for tw in range(2):
                    sa = w_in_d[rows, 3328 + tw * 1024 + c0 * 128: 3328 + tw * 1024 + c0 * 128 + 512].rearrange("p (ct c) -> p ct c", ct=4, c=128)
                    vf = (lambda tw: lambda t_: t_[:, 0:1024].rearrange("p (ct two c) -> p ct two c", ct=4, two=2, c=128)[:, :, tw, :])(tw)
                    loads.append((vf, sa))
                cast_chunk(None, s_in[rows, 3328 + c0 * 256: 3328 + c0 * 256 + 1024], 1024, loads=loads)
            for c0 in range(5376, 7424, 1024):
                cast_chunk(w_in_d[rows, c0:c0 + 1024], s_in[rows, c0:c0 + 1024], 1024)
            for wd_, sd_ in [(w_pa_d, s_pa), (w_pb_d, s_pb), (w_o_d, s_o)]:
                cast_chunk(wd_[rows, :], sd_[rows, :], 1024)
            for c0 in range(0, DFF, 1024):
                cast_chunk(w_up_d[rows, c0:c0 + 1024], s_up[rows, c0:c0 + 1024], 1024)
        for kt in range(32):
            rows = slice(kt * 128, (kt + 1) * 128)
            cast_chunk(w_dn_d[rows, :], s_dn[rows, :], 1024)

        S.ckpt('setup')
        ring_i = [0]
        cur = {}

        def wload(tag, src_ap, nk, ncol):
            i = ring_i[0] % NRING
            ring_i[0] += 1
            view = wring[i][:, 0:nk * ncol].rearrange("p (k c) -> p k c", k=nk, c=ncol)
            dma("sync", view, src_ap.rearrange("(k p) c -> p k c", p=128), reads=list(scr_keys), writes=[f"wring{i}"])
            return view, f"wring{i}"

        def mm(out, lhsT, rhs, start, stop, reads, writes, tp=None):
            if tp is None:
                S.op("tensor", lambda e: e.matmul(out, lhsT=lhsT, rhs=rhs, start=start, stop=stop), reads=reads, writes=writes)
            else:
                S.op("tensor", lambda e: e.matmul(out, lhsT=lhsT, rhs=rhs, start=start, stop=stop, tile_position=tp), reads=reads, writes=writes)

        def act(out, in_, func, reads, writes, bias=None, scale=None, eng="scalar"):
            kw = {}
            if bias is not None:
                kw["bias"] = bias
            if scale is not None:
                kw["scale"] = scale
            S.op(eng, lambda e: e.activation(out=out, in_=in_, func=func, **kw), reads=reads, writes=writes)

        def tt(out, a, b, op, reads, writes, eng="vector"):
            S.op(eng, lambda e: e.tensor_tensor(out=out, in0=a, in1=b, op=op), reads=reads, writes=writes)

        def stt(out, a, sc, b, op0, op1, reads, writes, eng="vector"):
            S.op(eng, lambda e: e.scalar_tensor_tensor(out=out, in0=a, scalar=sc, in1=b, op0=op0, op1=op1), reads=reads, writes=writes)

        def ts(out, a, s1, s2, op0, op1, reads, writes, eng="vector"):
            if s2 is None:
                S.op(eng, lambda e: e.tensor_scalar(out=out, in0=a, scalar1=s1, scalar2=None, op0=op0), reads=reads, writes=writes)
            else:
                S.op(eng, lambda e: e.tensor_scalar(out=out, in0=a, scalar1=s1, scalar2=s2, op0=op0, op1=op1), reads=reads, writes=writes)

        def cp(out, in_, reads, writes, eng="vector"):
            S.op(eng, lambda e: e.tensor_copy(out=out, in_=in_), reads=reads, writes=writes)

        pm_i = [0]

        def next_pm():
            i = pm_i[0] % 3
            pm_i[0] += 1
            return P3[i], f"PM{i}"

        eps_col = sb("eps_col", [128, 2])
        S.op("gpsimd", lambda e: e.memset(eps_col[:, 0:1], 1e-6), writes=["eps_col"])
        S.op("gpsimd", lambda e: e.memset(eps_col[:, 1:2], 64e-5), writes=["eps_col"])

        def rms_stats():
            P, pk = next_pm()
            for kt in range(8):
                b_ = kt % 2
                act(Bt[b_][:], h[kt][:], AF.Square, reads=[f"h{kt}"], writes=[f"Bt{b_}"])
                mm(P[:], onesb[:], Bt[b_][:], kt == 0, kt == 7, reads=["onesb", f"Bt{b_}"], writes=[pk])
            act(T[15][:], P[:], AF.Sqrt, reads=[pk, "eps_col"], writes=["T15"], bias=eps_col[:, 0:1], scale=1.0 / D)
            S.op("vector", lambda e: e.reciprocal(out=T[15][:], in_=T[15][:]), reads=["T15"], writes=["T15"])

        def merge(*gens):
            gens = list(gens)
            while gens:
                for g in list(gens):
                    try:
                        next(g)
                    except StopIteration:
                        gens.remove(g)
                yield

        def seq(*gens):
            for g in gens:
                yield from g

        def drain(g):
            for _ in g:
                pass

        import os
        G_ = int(os.environ.get('KG', '1'))
        R1bank = [WA, WA, WB, WB]

        def R1(j4, a_, b_):
            off = (j4 % 2) * 256
            return R1bank[j4][:, off + a_:off + b_]

        def R1l(j4, lr, a_, b_):
            off = (j4 % 2) * 256
            return R1bank[j4][lr, off + a_:off + b_]

        def R1k(j4, qs):
            return [("R1", j4)]

        LN = [(slice(0, 64), (0, 0)), (slice(64, 128), (64, 64))]

        for blk in range(nblk):
            t0 = blk * TB
            for kt in range(8):
                dma("sync", h[kt][:], xT[kt * 128:(kt + 1) * 128, t0:t0 + TB], writes=[f"h{kt}"])
            rms_stats()
            for kt in range(8):
                stt(ub[kt][:], h[kt][:], C("g1", kt), T[15][:], ALU.mult, ALU.mult, reads=[f"h{kt}", "T15", "cols"], writes=[f"ub{kt}"])

            wgrp = {}

            def win_tile(nj, wgrp=wgrp):
                g = nj // 4
                if g not in wgrp or ring_i[0] - wgrp[g][2] >= NRING:
                    nt = min(4, 58 - g * 4)
                    seq_ = ring_i[0]
                    wgrp[g] = wload("win", s_in[:, g * 512:g * 512 + nt * 128], 8, nt * 128) + (seq_,)
                view, key, _ = wgrp[g]
                off = (nj % 4) * 128
                P, pk = next_pm()
                for kt in range(8):
                    mm(P[:], view[:, kt, off:off + 128], ub[kt][:], kt == 0, kt == 7, reads=[key, f"ub{kt}"], writes=[pk])
                return P, pk

            def shift_evac(nj, out_ap, out_key):
                P, pk = win_tile(nj)
                oj = ORDER[nj]
                b = nj % 2
                act(pm[b][:, 1:TB + 1], P[:], AF.Identity, reads=[pk, "cols"], writes=[f"pm{b}"], scale=C("mu", oj))
                act(pm[b][:, 0:1], carry[:, nj:nj + 1], AF.Identity, reads=["carry", f"pm{b}"], writes=[f"pm{b}"])
                stt(out_ap, P[:], OMU(oj), pm[b][:, 0:TB], ALU.mult, ALU.add, reads=[pk, "dcols", f"pm{b}"], writes=[out_key])
                act(carry[:, nj:nj + 1], pm[b][:, TB:TB + 1], AF.Identity, reads=[f"pm{b}"], writes=["carry"])

            shift_evac(0, T[0][:], "T0")
            act(tl[0:64, :], T[0][0:64, :], AF.Tanh, reads=["T0"], writes=["tl"])
            cp(tl[64:128, :], T[0][64:128, :], reads=["T0", "tl"], writes=["tl"])
            shift_evac(1, T[0][:], "T0")
            act(sgd[:], T[0][:], AF.Sigmoid, reads=["T0"], writes=["sgd"])

            def elem_gen(ct):
                q = ct % 2
                cs_ = slice(ct * 128, (ct + 1) * 128)
                ARq, BKq, vbq, gamq, bonq = AR[q], BK[q], vb[q], gam[q], bon[q]
                kAR, kBK, kvb, kgam, kbon = f"AR{q}", f"BK{q}", f"vb{q}", f"gam{q}", f"bon{q}"
                for qq in range(3):
                    shift_evac(2 + ct * 3 + qq, rkv[qq][:], f"rkv{qq}")
                    yield
                r_, k_, v_ = rkv[0], rkv[1], rkv[2]
                P0, k0 = next_pm()
                mm(P0[:], lorab[0:64, cs_], tl[0:64, :], True, True, reads=["lorab", "tl"], writes=[k0])
                P1, k1 = next_pm()
                mm(P1[:], lorab[64:128, cs_], tl[64:128, :], True, True, reads=["lorab", "tl"], writes=[k1], tp=(64, 0))
                act(T[1][:], P0[:], AF.Sigmoid, reads=[k0, "cols"], writes=["T1"], bias=C("w0", ct), scale=1.0)
                act(T[2][:], P1[:], AF.Sigmoid, reads=[k1, "cols"], writes=["T2"], bias=C("a0", ct), scale=1.0)
                yield
                S.op("vector", lambda e: e.tensor_tensor_scan(out=T[3][:], data0=scanmask, data1=T[1][:], initial=0.0, op0=ALU.mult, op1=ALU.add), reads=["cst", "T1"], writes=["T3"])
                act(T[4][:], T[3][:], AF.Exp, reads=["T3"], writes=["T4"], scale=-EDEC)
                act(T[5][:], T[3][:], AF.Exp, reads=["T3"], writes=["T5"], scale=EDEC)
                yield
                tt(T[6][:], T[3][:], T[1][:], ALU.subtract, reads=["T3", "T1"], writes=["T6"], eng="gpsimd")
                act(T[6][:], T[6][:], AF.Exp, reads=["T6"], writes=["T6"], scale=-EDEC)
                cp(gamq[:], T[4][:].rearrange("p (c t) -> p c t", t=CH)[:, :, CH - 1], reads=["T4"], writes=[kgam], eng="gpsimd")
                yield
                ts(T[7][:], k_[:], C("kk", ct), None, ALU.mult, None, reads=["rkv1", "cols"], writes=["T7"], eng="gpsimd")
                act(Bt[0][:], T[7][:], AF.Square, reads=["T7"], writes=["Bt0"])
                P0, k0 = next_pm()
                mm(P0[:], blkb[:], Bt[0][:], True, True, reads=["blkb", "Bt0"], writes=[k0])
                act(T[8][:], P0[:], AF.Sqrt, reads=[k0], writes=["T8"])
                yield
                ts(T[8][:], T[8][:], 1e-12, None, ALU.max, None, reads=["T8"], writes=["T8"], eng="gpsimd")
                S.op("vector", lambda e: e.reciprocal(out=T[8][:], in_=T[8][:]), reads=["T8"], writes=["T8"])
                tt(T[7][:], T[7][:], T[8][:], ALU.mult, reads=["T7", "T8"], writes=["T7"], eng="gpsimd")
                yield
                ts(T[9][:], T[2][:], C("ka", ct), OMKA(ct), ALU.mult, ALU.add, reads=["T2", "cols", "dcols"], writes=["T9"], eng="gpsimd")
                tt(T[9][:], T[9][:], k_[:], ALU.mult, reads=["T9", "rkv1"], writes=["T9"], eng="gpsimd")
                stt(Bt[1][:], r_[:], C("rk", ct), T[9][:], ALU.mult, ALU.mult, reads=["rkv0", "cols", "T9"], writes=["Bt1"])
                P1, k1 = next_pm()
                mm(P1[:], blkb[:], Bt[1][:], True, True, reads=["blkb", "Bt1"], writes=[k1])
                tt(bonq[:], P1[:], v_[:], ALU.mult, reads=[k1, "rkv2"], writes=[kbon])
                yield
                v3 = lambda t_: t_[:].rearrange("p (c t) -> p c t", t=CH)
                tt(ARq[:, :, 64:128], v3(r_), v3(T[4]), ALU.mult, reads=["rkv0", "T4"], writes=[kAR])
                stt(ARq[:, :, 0:64], v3(T[7]), -1.0, v3(T[6]), ALU.mult, ALU.mult, reads=["T7", "T6"], writes=[kAR])
                yield
                tt(BKq[:, :, 64:128], v3(T[9]), v3(T[5]), ALU.mult, reads=["T9", "T5"], writes=[kBK], eng="gpsimd")
                tt(T[8][:], T[7][:], T[2][:], ALU.mult, reads=["T7", "T2"], writes=["T8"], eng="gpsimd")
                tt(BKq[:, :, 0:64], v3(T[8]), v3(T[5]), ALU.mult, reads=["T8", "T5"], writes=[kBK])
                act(vbq[:], v_[:], AF.Identity, reads=["rkv2"], writes=[kvb])
                yield

            def prep_gen(ct, grp):
                q = ct % 2
                ARq, BKq, vbq = AR[q], BK[q], vb[q]
                kAR, kBK, kvb = f"AR{q}", f"BK{q}", f"vb{q}"
                chunks = list(range(grp * G_, grp * G_ + G_))
                for c in chunks:
                    j, j4 = c, c % 4
                    ccol = slice(c * CH, (c + 1) * CH)
                    srcs = [(ARq[:, c, 0:64], kAR, 0), (BKq[:, c, 0:64], kBK, 64), (BKq[:, c, 64:128], kBK, 128), (vbq[:, ccol], kvb, 192)]
                    for (src_, sk, off) in srcs:
                        for lr, tp in LN:
                            S.op("tensor", (lambda src_, lr, off, j4, tp: lambda e: e.transpose(PT[lr, j4 * 256 + off:j4 * 256 + off + 64], src_[lr], identb[lr, lr], tile_position=tp))(src_, lr, off, j4, tp),
                                 reads=[sk, "identb"], writes=[("PT", j4)])
                yield
                for c in chunks:
                    j, j4 = c, c % 4
                    acopy(TM[j][:, 0:64], PT[:, j4 * 256:j4 * 256 + 64], reads=[("PT", j4)], writes=[f"TM{j}"])
                    acopy(TM[j][:, 128:320], PT[:, j4 * 256 + 64:j4 * 256 + 256], reads=[("PT", j4)], writes=[f"TM{j}"])
                    acopy(XF[j4][:, 0:64], PT[:, j4 * 256:j4 * 256 + 64], reads=[("PT", j4)], writes=[f"XF{j4}"])
                S.ckpt('p_tr')
                for c in chunks:
                    j, j4 = c, c % 4
                    for lr, tp in LN:
                        mm(R1l(j4, lr, 0, 128), BKq[lr, c, 0:64], ARq[lr, c, :], True, True, reads=[kBK, kAR], writes=R1k(j4, [0, 1]), tp=tp)
                        mm(R1l(j4, lr, 128, 256), BKq[lr, c, 64:128], ARq[lr, c, :], True, True, reads=[kBK, kAR], writes=R1k(j4, [2, 3]), tp=tp)
                        mm(WC[lr, j4 * 128:j4 * 128 + 64], ARq[lr, c, 0:64], BKq[lr, c, 0:64], True, True, reads=[kBK, kAR], writes=[("R2", j4)], tp=tp)
                yield
                for c in chunks:
                    j, j4 = c, c % 4
                    tt(PRm[j][:, 0:256], R1(j4, 0, 256), mask320[:, 0:256], ALU.mult, reads=R1k(j4, [0, 1, 2, 3]) + ["cst"], writes=[f"PRm{j}"])
                    tt(PRm[j][:, 256:320], WC[:, j4 * 128:j4 * 128 + 64], mask320[:, 256:320], ALU.mult, reads=[("R2", j4), "cst"], writes=[f"PRm{j}"])
                S.ckpt('p_prod')
                for c in chunks:
                    j, j4 = c, c % 4
                    for lr, tp in LN:
                        mm(WC[lr, j4 * 128 + 64:j4 * 128 + 128], PRm[j][lr, 128:192], TM[j][lr, 256:320], True, True, reads=[f"PRm{j}", f"TM{j}"], writes=[("R2", j4)], tp=tp)
                yield
                for c in chunks:
                    j, j4 = c, c % 4
                    acopy(TM[j][:, 64:128], WC[:, j4 * 128 + 64:j4 * 128 + 128], reads=[("R2", j4)], writes=[f"TM{j}"])
                    acopy(XF[j4][:, 64:128], WC[:, j4 * 128 + 64:j4 * 128 + 128], reads=[("R2", j4)], writes=[f"XF{j4}"])
                S.ckpt('p_aak')
                st = {}
                for c in chunks:
                    st[c] = dict(X=TM[c][:, 0:128], xk=f"TM{c}", N=PRm[c][:, 256:320], NT=PRm[c][:, 0:64], nk=f"PRm{c}")
                for lvl in range(6):
                    for c in chunks:
                        j, j4 = c, c % 4
                        s_ = st[c]
                        for lr, tp in LN:
                            mm(R1l(j4, lr, 0, 128), s_["NT"][lr], s_["X"][lr], True, True, reads=[s_["nk"], s_["xk"]], writes=R1k(j4, [0, 1]), tp=tp)
                        if lvl < 5:
                            for lr, tp in LN:
                                mm(R1l(j4, lr, 128, 192), s_["NT"][lr], s_["N"][lr], True, True, reads=[s_["nk"]], writes=R1k(j4, [2]), tp=tp)
                                mm(R1l(j4, lr, 192, 256), s_["N"][lr], s_["NT"][lr], True, True, reads=[s_["nk"]], writes=R1k(j4, [3]), tp=tp)
                    yield
                    for c in chunks:
                        j, j4 = c, c % 4
                        s_ = st[c]
                        nx = XB[j][lvl % 2]
                        tt(XF[j4][:], R1(j4, 0, 128), XF[j4][:], ALU.add, reads=R1k(j4, [0, 1]) + [f"XF{j4}"], writes=[f"XF{j4}"])
                        acopy(nx[:], XF[j4][:], reads=[f"XF{j4}"], writes=[f"XB{j}_{lvl % 2}"])
                        s_["X"], s_["xk"] = nx[:], f"XB{j}_{lvl % 2}"
                        if lvl < 5:
                            nn = NN[j4][lvl % 2]
                            acopy(nn[:], R1(j4, 128, 256), reads=R1k(j4, [2, 3]), writes=[f"NN{j4}_{lvl % 2}"])
                            s_["N"], s_["NT"], s_["nk"] = nn[:, 0:64], nn[:, 64:128], f"NN{j4}_{lvl % 2}"
                S.ckpt('p_dbl')
                for c in chunks:
                    j, j4 = c, c % 4
                    s_ = st[c]
                    Wp, U0 = s_["X"][:, 0:64], s_["X"][:, 64:128]
                    Btm, Ktm, Vtm = TM[j][:, 128:192], TM[j][:, 192:256], TM[j][:, 256:320]
                    ArbT = PRm[j][:, 64:128]
                    for lr, tp in LN:
                        mm(R1l(j4, lr, 0, 64), Wp[lr], Btm[lr], True, True, reads=[s_["xk"], f"TM{j}"], writes=R1k(j4, [0]), tp=tp)
                        mm(R1l(j4, lr, 64, 128), Btm[lr], U0[lr], True, False, reads=[s_["xk"], f"TM{j}"], writes=R1k(j4, [1]), tp=tp)
                        mm(R1l(j4, lr, 64, 128), Ktm[lr], Vtm[lr], False, True, reads=[f"TM{j}"], writes=R1k(j4, [1]), tp=tp)
                        mm(R1l(j4, lr, 128, 192), Wp[lr], ArbT[lr], True, False, reads=[s_["xk"], f"PRm{j}"], writes=R1k(j4, [2]), tp=tp)
                        mm(R1l(j4, lr, 128, 192), identb[lr, lr], ARq[lr, c, 64:128], False, True, reads=["identb", kAR], writes=R1k(j4, [2]), tp=tp)
                yield
                for c in chunks:
                    j, j4 = c, c % 4
                    acopy(ETb[j][:], R1(j4, 0, 64), reads=R1k(j4, [0]), writes=[f"ETb{j}"])
                    act(GG[j][:], R1(j4, 64, 128), AF.Identity, reads=R1k(j4, [1]) + [f"gam{q}"], writes=[f"GG{j}"], scale=gam[q][:, c:c + 1])
                    acopy(QTb[j][:], R1(j4, 128, 192), reads=R1k(j4, [2]), writes=[f"QTb{j}"])
                    st[c]["done"] = True
                yield
                prep_state[(ct, grp)] = st
                S.ckpt('p_fin')

            prep_state = {}

            def chain_gen(ct, grp):
                q = ct % 2
                hs_ = slice(ct * 64, (ct + 1) * 64)
                st = prep_state[(ct, grp)]
                for c in range(grp * G_, grp * G_ + G_):
                    j = c
                    s_ = st[c]
                    U0 = s_["X"][:, 64:128]
                    Vtm = TM[j][:, 256:320]
                    ArbT, ArkT = PRm[j][:, 64:128], PRm[j][:, 192:256]
                    ccol = slice(c * CH, (c + 1) * CH)
                    sp = c % 2
                    for lr, tp in LN:
                        mm(PY[lr, ccol], U0[lr], ArbT[lr], True, False, reads=[s_["xk"], f"PRm{j}"], writes=["PY"], tp=tp)
                        mm(PY[lr, ccol], Vtm[lr], ArkT[lr], False, False, reads=[f"TM{j}", f"PRm{j}"], writes=["PY"], tp=tp)
                        mm(PY[lr, ccol], Hb[sp][lr, hs_], QTb[j][lr], False, True, reads=[f"Hb{sp}", f"QTb{j}"], writes=["PY"], tp=tp)
                    Pc, kc = next_pm()
                    for lr, tp in LN:
                        mm(Pc[lr, 0:64], ETb[j][lr], Hb[sp][lr, hs_], True, True, reads=[f"ETb{j}", f"Hb{sp}"], writes=[kc], tp=tp)
                    tt(tch[:], Pc[:, 0:64], Hf[:, hs_], ALU.add, reads=[kc, "Hf"], writes=["tch"])
                    stt(Hf[:, hs_], tch[:], gam[q][:, c:c + 1], GG[j][:], ALU.mult, ALU.add, reads=["tch", f"gam{q}", f"GG{j}"], writes=["Hf"])
                    acopy(Hb[1 - sp][:, hs_], Hf[:, hs_], reads=["Hf"], writes=[f"Hb{1 - sp}"])
                    yield

            def gn_gen(ct):
                q = ct % 2
                cs_ = slice(ct * 128, (ct + 1) * 128)
                act(T[11][:], PY[:], AF.Identity, reads=["PY"], writes=["T11"])
                act(T[12][:], PY[:], AF.Square, reads=["PY"], writes=["T12"])
                P0, k0 = next_pm()
                mm(P0[:], blkf, T[11][:], True, True, reads=["cst", "T11"], writes=[k0])
                P1, k1 = next_pm()
                mm(P1[:], blkf, T[12][:], True, True, reads=["cst", "T12"], writes=[k1])
                ts(T[13][:], P0[:], 1.0 / 64, None, ALU.mult, None, reads=[k0], writes=["T13"])
                tt(T[14][:], T[13][:], T[13][:], ALU.mult, reads=["T13"], writes=["T14"], eng="gpsimd")
                stt(T[12][:], P1[:], 1.0 / 64, T[14][:], ALU.mult, ALU.subtract, reads=[k1, "T14"], writes=["T12"])
                act(T[12][:], T[12][:], AF.Sqrt, reads=["T12", "eps_col"], writes=["T12"], bias=eps_col[:, 1:2], scale=1.0)
                yield
                S.op("vector", lambda e: e.reciprocal(out=T[12][:], in_=T[12][:]), reads=["T12"], writes=["T12"])
                tt(T[11][:], T[11][:], T[13][:], ALU.subtract, reads=["T11", "T13"], writes=["T11"], eng="gpsimd")
                tt(T[11][:], T[11][:], T[12][:], ALU.mult, reads=["T11", "T12"], writes=["T11"])
                yield
                act(T[11][:], T[11][:], AF.Identity, reads=["T11", "cols"], writes=["T11"], bias=C("lnb", ct), scale=C("lnw", ct))
                tt(T[11][:], T[11][:], bon[q][:], ALU.add, reads=["T11", f"bon{q}"], writes=["T11"], eng="gpsimd")
                P2, k2 = next_pm()
                mm(P2[:], gupb[:, cs_], sgd[:], True, True, reads=["gupb", "sgd"], writes=[k2])
                tt(ygb[ct][:], P2[:], T[11][:], ALU.mult, reads=[k2, "T11"], writes=[f"ygb{ct}"])
                yield

            def wkv_gen(ct):
                ng = NCH // G_
                yield from prep_gen(ct, 0)
                for g_ in range(ng - 1):
                    yield from merge(chain_gen(ct, g_), prep_gen(ct, g_ + 1))
                yield from chain_gen(ct, ng - 1)
                yield from gn_gen(ct)

            def lru_gen(ct):
                cs_ = slice(ct * 128, (ct + 1) * 128)
                L = [T[0], T[1], T[2], T[3], T[4], T[5], T[6]]
                P, pk = win_tile(26 + ct * 2)
                act(xbh[:, 3:TB + 3], P[:], AF.Identity, reads=[pk], writes=["xbh"])
                cp(xbh[:, 0:3], hist[:, ct * 3:ct * 3 + 3], reads=["hist", "xbh"], writes=["xbh"], eng="gpsimd")
                cp(hist[:, ct * 3:ct * 3 + 3], xbh[:, TB:TB + 3], reads=["xbh"], writes=["hist"], eng="gpsimd")
                yield
                P2, pk2 = win_tile(26 + ct * 2 + 1)
                act(T[0][:], P2[:], AF.Gelu, reads=[pk2], writes=["T0"])
                yield
                act(T[1][:], xbh[:, 3:TB + 3], AF.Identity, reads=["xbh", "cols"], writes=["T1"], bias=C("cb", ct), scale=C("cw3", ct))
                for j in range(3):
                    stt(T[1][:], xbh[:, j:j + TB], C(f"cw{j}", ct), T[1][:], ALU.mult, ALU.add, reads=["xbh", "cols", "T1"], writes=["T1"])
                yield
                act(Bt[2][:], T[1][:], AF.Identity, reads=["T1"], writes=["Bt2"])
                P0, k0 = next_pm()
                mm(P0[:], wab[:, cs_], Bt[2][:], True, True, reads=["wab", "Bt2"], writes=[k0])
                P1, k1 = next_pm()
                mm(P1[:], wxb[:, cs_], Bt[2][:], True, True, reads=["wxb", "Bt2"], writes=[k1])
                act(T[2][:], P0[:], AF.Sigmoid, reads=[k0, "cols"], writes=["T2"], bias=C("ba", ct), scale=1.0)
                act(T[3][:], P1[:], AF.Sigmoid, reads=[k1, "cols"], writes=["T3"], bias=C("bx", ct), scale=1.0)
                yield
                act(T[4][:], T[2][:], AF.Exp, reads=["T2", "dcols"], writes=["T4"], scale=CL(ct))
                act(T[5][:], T[2][:], AF.Exp, reads=["T2", "dcols"], writes=["T5"], scale=CL2(ct))
                ts(T[5][:], T[5][:], -1.0, 1.0, ALU.mult, ALU.add, reads=["T5"], writes=["T5"], eng="gpsimd")
                act(T[5][:], T[5][:], AF.Sqrt, reads=["T5"], writes=["T5"])
                yield
                tt(T[3][:], T[3][:], T[1][:], ALU.mult, reads=["T3", "T1"], writes=["T3"], eng="gpsimd")
                tt(T[3][:], T[3][:], T[5][:], ALU.mult, reads=["T3", "T5"], writes=["T3"], eng="gpsimd")
                S.op("vector", (lambda ct: lambda e: e.tensor_tensor_scan(out=T[6][:], data0=T[4][:], data1=T[3][:], initial=lstate[:, ct:ct + 1], op0=ALU.mult, op1=ALU.add))(ct),
                     reads=["T4", "T3", "lstate"], writes=["T6"])
                cp(lstate[:, ct:ct + 1], T[6][:, TB - 1:TB], reads=["T6"], writes=["lstate"], eng="gpsimd")
                tt(lob[ct][:], T[6][:], T[0][:], ALU.mult, reads=["T6", "T0"], writes=[f"lob{ct}"], eng="gpsimd")
                yield

            def gates_gen():
                for g in range(16):
                    P, pk = win_tile(42 + g)
                    act(gb[g][:], P[:], AF.Sigmoid, reads=[pk], writes=[f"gb{g}"])
                    yield

            drain(elem_gen(0))
            S.ckpt("elem0")
            for ct in range(8):
                if ct < 7:
                    aux = seq(elem_gen(ct + 1), lru_gen(ct))
                else:
                    aux = seq(lru_gen(7), gates_gen())
                import os
                if os.environ.get('KNOAUX'):
                    drain(wkv_gen(ct)); drain(aux)
                else:
                    drain(merge(wkv_gen(ct), aux))
                S.ckpt(f"step{ct}")

            for ot in range(8):
                if ot % 4 == 0:
                    wa_v, wa_k = wload("pa", s_pa[:, ot * 128:ot * 128 + 512], 8, 512)
                    wb_v, wb_k = wload("pb", s_pb[:, ot * 128:ot * 128 + 512], 8, 512)
                os_ = slice((ot % 4) * 128, (ot % 4 + 1) * 128)
                Pa, pka = next_pm()
                for kt in range(8):
                    mm(Pa[:], wa_v[:, kt, os_], ygb[kt][:], kt == 0, kt == 7, reads=[wa_k, f"ygb{kt}"], writes=[pka])
                Pb, pkb = next_pm()
                for kt in range(8):
                    mm(Pb[:], wb_v[:, kt, os_], lob[kt][:], kt == 0, kt == 7, reads=[wb_k, f"lob{kt}"], writes=[pkb])
                b_ = 2 * (ot % 2)
                acopy(Bt[2][:], Pa[:], reads=[pka], writes=["Bt2"])
                acopy(Bt[3][:], Pb[:], reads=[pkb], writes=["Bt3"])
                tt(T[b_][:], Bt[2][:], gb[ot][:], ALU.mult, reads=["Bt2", f"gb{ot}"], writes=[f"T{b_}"])
                tt(T[b_ + 1][:], Bt[3][:], gb[8 + ot][:], ALU.mult, reads=["Bt3", f"gb{8 + ot}"], writes=[f"T{b_ + 1}"])
                tt(ub[ot][:], T[b_][:], T[b_ + 1][:], ALU.add, reads=[f"T{b_}", f"T{b_ + 1}"], writes=[f"ub{ot}"], eng="gpsimd")
            for ot in range(8):
                if ot % 4 == 0:
                    wo_v, wo_k = wload("po", s_o[:, ot * 128:ot * 128 + 512], 8, 512)
                os_ = slice((ot % 4) * 128, (ot % 4 + 1) * 128)
                P, pk = next_pm()
                for kt in range(8):
                    mm(P[:], wo_v[:, kt, os_], ub[kt][:], kt == 0, kt == 7, reads=[wo_k, f"ub{kt}"], writes=[pk])
                tt(h[ot][:], P[:], h[ot][:], ALU.add, reads=[pk, f"h{ot}"], writes=[f"h{ot}"])
            S.ckpt("mix")
            rms_stats()
            for kt in range(8):
                stt(ub[kt][:], h[kt][:], C("g2", kt), T[15][:], ALU.mult, ALU.mult, reads=[f"h{kt}", "T15", "cols"], writes=[f"ub{kt}"])
            for half in range(2):
                for q8 in range(4):
                    wu_v, wu_k = wload("up", s_up[:, half * 2048 + q8 * 512:half * 2048 + (q8 + 1) * 512], 8, 512)
                    for f4 in range(4):
                        fs = slice(f4 * 128, (f4 + 1) * 128)
                        fi = q8 * 4 + f4
                        P, pk = next_pm()
                        for kt in range(8):
                            mm(P[:], wu_v[:, kt, fs], ub[kt][:], kt == 0, kt == 7, reads=[wu_k, f"ub{kt}"], writes=[pk])
                        ti = fi % 4
                        act(T[ti][:], P[:], AF.Relu, reads=[pk], writes=[f"T{ti}"])
                        tt(gb[fi][:], T[ti][:], T[ti][:], ALU.mult, reads=[f"T{ti}"], writes=[f"gb{fi}"], eng=["gpsimd", "vector"][fi % 2])
                for og in range(4):
                    wd_v, wd_k = wload("dn", s_dn[half * 2048:(half + 1) * 2048, og * 256:(og + 1) * 256], 16, 256)
                    for o2 in range(2):
                        ot = og * 2 + o2
                        P, pk = next_pm()
                        for ft in range(16):
                            mm(P[:], wd_v[:, ft, o2 * 128:(o2 + 1) * 128], gb[ft][:], ft == 0, ft == 15, reads=[wd_k, f"gb{ft}"], writes=[pk])
                        tt(h[ot][:], P[:], h[ot][:], ALU.add, reads=[pk, f"h{ot}"], writes=[f"h{ot}"])
            rms_stats()
            for kt in range(8):
                i = kt % 2
                stt(ost[i][:], h[kt][:], C("gf", kt), T[15][:], ALU.mult, ALU.mult, reads=[f"h{kt}", "T15", "cols"], writes=[f"ost{i}"])
                dma("gpsimd", outT[kt * 128:(kt + 1) * 128, t0:t0 + TB], ost[i][:], reads=[f"ost{i}"])

        S.plan()
        sems = {k: es.enter_context(nc.semaphore("s_" + "_".join(map(str, k)))) for k in S.sem_keys()}
        with nc.Block() as block:
            S.emit(block, sems)
    return nc


def _colpack(v, n):
    v = np.asarray(v, np.float32).reshape(n, 128)
    return np.ascontiguousarray(v.T)


def host_prep(inputs, nblk):
    TP = nblk * TB
    L = 0
    cols = np.zeros((128, NCOLS), np.float32)

    def put(name, v, n):
        cols[:, COLS[name]:COLS[name] + n] = _colpack(v, n)
    put("g1", inputs["norm_mix_g"][L], 8)
    put("mu", inputs["mu_shift"][L], 26)
    put("w0", inputs["w0"][L], 8)
    put("a0", inputs["a0"][L], 8)
    put("kk", inputs["k_k"][L], 8)
    put("ka", inputs["k_a"][L], 8)
    put("rk", inputs["r_k"][L].reshape(-1), 8)
    put("lnw", inputs["ln_x_w"][L], 8)
    put("lnb", inputs["ln_x_b"][L], 8)
    for j in range(4):
        put(f"cw{j}", inputs["conv_w"][L][j], 8)
    put("cb", inputs["conv_b"][L], 8)
    put("ba", inputs["lru_ba"][L], 8)
    put("bx", inputs["lru_bx"][L], 8)
    put("lam", inputs["lru_lambda"][L], 8)
    put("g2", inputs["norm_ffn_g"][L], 8)
    put("gf", inputs["norm_final_g"], 8)
    ident = np.eye(128, dtype=np.float32)
    blk = np.kron(np.eye(2, dtype=np.float32), np.ones((64, 64), np.float32))
    ones = np.ones((128, 128), np.float32)
    s = np.arange(64)[:, None]
    t = np.arange(64)[None, :]
    strict = (s < t).astype(np.float32)
    incl = (s <= t).astype(np.float32)
    strictT = (s > t).astype(np.float32)
    m1 = np.concatenate([strict, incl, strict, incl, strictT], axis=1)
    mask320 = np.concatenate([m1, m1], axis=0)
    scanmask = np.ones((128, TB), np.float32)
    scanmask[:, ::CH] = 0.0
    consts = np.ascontiguousarray(np.concatenate([ident, blk, ones, mask320, scanmask], axis=1))
    lora = np.ascontiguousarray(np.concatenate([inputs["w_decay_up"][L], inputs["w_aaa_up"][L]], axis=0).astype(np.float32))
    shared = dict(
        cols=cols, consts=consts,
        w_in=np.ascontiguousarray(inputs["w_in"][L], dtype=np.float32),
        w_pa=np.ascontiguousarray(inputs["w_proj_a"][L], dtype=np.float32),
        w_pb=np.ascontiguousarray(inputs["w_proj_b"][L], dtype=np.float32),
        w_o=np.ascontiguousarray(inputs["w_out"][L], dtype=np.float32),
        w_up=np.ascontiguousarray(inputs["w_ff_up"][L], dtype=np.float32),
        w_dn=np.ascontiguousarray(inputs["w_ff_down"][L], dtype=np.float32),
        lora=lora,
        gup=np.ascontiguousarray(inputs["w_gate_up"][L], dtype=np.float32),
        lwa=np.ascontiguousarray(inputs["lru_wa"][L], dtype=np.float32),
        lwx=np.ascontiguousarray(inputs["lru_wx"][L], dtype=np.float32),
    )
    x = np.asarray(inputs["x"], np.float32)
    meta = np.asarray(inputs["meta_tokens"], np.float32)
    B, Tx, _ = x.shape
    xts = []
    for b in range(B):
        hfull = np.zeros((TP, D), np.float32)
        n = min(TP, NMETA + Tx)
        hfull[:NMETA] = meta
        hfull[NMETA:n] = x[b, :n - NMETA]
        xts.append(np.ascontiguousarray(hfull.T))
    return shared, xts


_NC_CACHE = {}


def run(inputs, nblk, ncores=8):
    shared, xts = host_prep(inputs, nblk)
    if nblk not in _NC_CACHE:
        _NC_CACHE[nblk] = build(nblk)
    nc = _NC_CACHE[nblk]
    B = len(xts)
    in_maps = [dict(shared, xT=xts[i % B]) for i in range(ncores)]
    res = run_bass_kernel_spmd(nc, in_maps, core_ids=list(range(ncores)))
    outs = [np.ascontiguousarray(res.results[b]["outT"].T) for b in range(B)]
    return np.stack(outs, axis=0)


def kernel(**inputs):
    nblk = (NMETA + SEQ + TB - 1) // TB
    full = run(inputs, nblk)
    return np.ascontiguousarray(full[:, NMETA:NMETA + SEQ, :]).astype(np.float32)
```
